# Optimizing a Trainium2 kernel written in Bass

```python
import math
import jax, jax.numpy as jnp
from jax import lax
import numpy as np

D_MODEL = 2048
BATCH = 2
SEQ = 4096
DEPTH = 1

MLA_HEADS = 8
MLA_Q_RANK = 512
MLA_KV_RANK = 512
MLA_NOPE_DIM = 128
MLA_ROPE_DIM = 64
MLA_V_DIM = 128
DIFF_HEADS = 8
DIFF_QK_DIM = 64
DIFF_V_DIM = 2 * DIFF_QK_DIM
D_FF = -(-8 * D_MODEL // (3 * 256)) * 256
ROPE_THETA = 10000.0
Q_BLOCK = 128
ALPHA = (2 * DEPTH) ** 0.25
BETA = (8 * DEPTH) ** -0.25
RMS_EPS = 1e-6
SUBLN_EPS = 1e-5
LN_EPS = 1e-5

IN_SIZES = (
    MLA_Q_RANK,
    MLA_KV_RANK,
    MLA_ROPE_DIM,
    DIFF_HEADS * 2 * DIFF_QK_DIM,
    DIFF_HEADS * 2 * DIFF_QK_DIM,
    DIFF_HEADS * DIFF_V_DIM,
    D_MODEL,
    D_MODEL,
)
IN_WIDTH = sum(IN_SIZES)

kernel_name = 'hybrid_mla_diffattn_deepnorm'


def _rms_norm(x, g, eps):
    xf = x.astype(jnp.float32)
    y = xf * lax.rsqrt(jnp.mean(xf * xf, axis=-1, keepdims=True) + eps)
    return (y * g.astype(jnp.float32)).astype(x.dtype)


def _layer_norm(x, g, b):
    xf = x.astype(jnp.float32)
    mu = jnp.mean(xf, axis=-1, keepdims=True)
    xc = xf - mu
    var = jnp.mean(xc * xc, axis=-1, keepdims=True)
    y = xc * lax.rsqrt(var + LN_EPS) * g.astype(jnp.float32) + b.astype(jnp.float32)
    return y.astype(x.dtype)


def _rope_tables(seq, dim):
    inv_freq = 1.0 / (ROPE_THETA ** (jnp.arange(0, dim, 2, dtype=jnp.float32) / dim))
    ang = jnp.arange(seq, dtype=jnp.float32)[:, None] * inv_freq[None, :]
    return jnp.cos(ang), jnp.sin(ang)


def _apply_rope(x, cos, sin):
    xf = x.astype(jnp.float32)
    x1, x2 = jnp.split(xf, 2, axis=-1)
    shape = (1, cos.shape[0]) + (1,) * (x.ndim - 3) + (cos.shape[1],)
    c = cos.reshape(shape)
    s = sin.reshape(shape)
    return jnp.concatenate([x1 * c - x2 * s, x2 * c + x1 * s], axis=-1).astype(x.dtype)


def _causal_multimap_attention(q, k, v, coef, scale):
    B, S, H, M, Dk = q.shape
    Dv = v.shape[-1]
    nb = S // Q_BLOCK
    q_blocks = jnp.moveaxis(q.reshape(B, nb, Q_BLOCK, H, M, Dk), 1, 0)
    k_pos = jnp.arange(S)
    coef32 = coef.astype(jnp.float32)

    def block(args):
        qb, i = args
        s = jnp.einsum('bqhmd,bkhmd->bhmqk', qb, k,
                       preferred_element_type=jnp.float32) * scale
        q_pos = i * Q_BLOCK + jnp.arange(Q_BLOCK)
        causal = k_pos[None, :] <= q_pos[:, None]
        s = jnp.where(causal, s, -jnp.inf)
        p = jax.nn.softmax(s, axis=-1)
        w = jnp.einsum('bhmqk,m->bhqk', p, coef32)
        o = jnp.einsum('bhqk,bkhd->bqhd', w.astype(v.dtype), v,
                       preferred_element_type=jnp.float32)
        return o.astype(v.dtype)

    out = lax.map(block, (q_blocks, jnp.arange(nb)))
    return jnp.moveaxis(out, 0, 1).reshape(B, S, H, Dv)


def _mla_branch(c_q, c_kv, k_rope, q_norm_g, w_uq, kv_norm_g, w_ukv, cos, sin):
    B, S, _ = c_q.shape
    q = (_rms_norm(c_q, q_norm_g, RMS_EPS) @ w_uq).reshape(B, S, MLA_HEADS, MLA_NOPE_DIM + MLA_ROPE_DIM)
    q_nope, q_pe = jnp.split(q, [MLA_NOPE_DIM], axis=-1)
    q_pe = _apply_rope(q_pe, cos, sin)
    kv = (_rms_norm(c_kv, kv_norm_g, RMS_EPS) @ w_ukv).reshape(B, S, MLA_HEADS, MLA_NOPE_DIM + MLA_V_DIM)
    k_nope, v = jnp.split(kv, [MLA_NOPE_DIM], axis=-1)
    k_pe = _apply_rope(k_rope, cos, sin)
    k_pe = jnp.broadcast_to(k_pe[:, :, None, :], (B, S, MLA_HEADS, MLA_ROPE_DIM))
    q_full = jnp.concatenate([q_nope, q_pe], axis=-1)[:, :, :, None, :]
    k_full = jnp.concatenate([k_nope, k_pe], axis=-1)[:, :, :, None, :]
    scale = (MLA_NOPE_DIM + MLA_ROPE_DIM) ** -0.5
    out = _causal_multimap_attention(q_full, k_full, v, jnp.ones((1,), jnp.float32), scale)
    return out.reshape(B, S, MLA_HEADS * MLA_V_DIM)


def _diff_branch(dq, dk, dv, lq1, lk1, lq2, lk2, subln_g, lambda_init, cos, sin):
    B, S, _ = dq.shape
    q = _apply_rope(dq.reshape(B, S, DIFF_HEADS, 2, DIFF_QK_DIM), cos, sin)
    k = _apply_rope(dk.reshape(B, S, DIFF_HEADS, 2, DIFF_QK_DIM), cos, sin)
    v = dv.reshape(B, S, DIFF_HEADS, DIFF_V_DIM)
    lam = (jnp.exp(jnp.sum(lq1.astype(jnp.float32) * lk1.astype(jnp.float32)))
           - jnp.exp(jnp.sum(lq2.astype(jnp.float32) * lk2.astype(jnp.float32)))
           + lambda_init)
    coef = jnp.stack([jnp.ones_like(lam), -lam])
    out = _causal_multimap_attention(q, k, v, coef, DIFF_QK_DIM ** -0.5)
    out = _rms_norm(out, subln_g, SUBLN_EPS) * (1.0 - lambda_init)
    return out.reshape(B, S, DIFF_HEADS * DIFF_V_DIM)


def setup_inputs(seed: int = 0) -> dict:
    key = jax.random.key(seed)
    ks = jax.random.split(key, 20)
    f32 = jnp.float32

    def nrm(k, shape, scale):
        return jax.random.normal(k, shape, f32) * scale

    def gain(k, shape):
        return 1.0 + 0.02 * jax.random.normal(k, shape, f32)

    L = DEPTH
    mla_out = MLA_HEADS * MLA_V_DIM
    diff_out = DIFF_HEADS * DIFF_V_DIM
    return {
        'x': jax.random.normal(ks[0], (BATCH, SEQ, D_MODEL), f32),
        'w_in': nrm(ks[1], (L, D_MODEL, IN_WIDTH), D_MODEL ** -0.5),
        'mla_q_norm': gain(ks[2], (L, MLA_Q_RANK)),
        'mla_w_uq': nrm(ks[3], (L, MLA_Q_RANK, MLA_HEADS * (MLA_NOPE_DIM + MLA_ROPE_DIM)), MLA_Q_RANK ** -0.5),
        'mla_kv_norm': gain(ks[4], (L, MLA_KV_RANK)),
        'mla_w_ukv': nrm(ks[5], (L, MLA_KV_RANK, MLA_HEADS * (MLA_NOPE_DIM + MLA_V_DIM)), MLA_KV_RANK ** -0.5),
        'diff_lambda_q1': nrm(ks[6], (L, DIFF_QK_DIM), 0.1),
        'diff_lambda_k1': nrm(ks[7], (L, DIFF_QK_DIM), 0.1),
        'diff_lambda_q2': nrm(ks[8], (L, DIFF_QK_DIM), 0.1),
        'diff_lambda_k2': nrm(ks[9], (L, DIFF_QK_DIM), 0.1),
        'diff_subln': gain(ks[10], (L, DIFF_V_DIM)),
        'w_branch_a': nrm(ks[11], (L, mla_out, D_MODEL), BETA * mla_out ** -0.5),
        'w_branch_b': nrm(ks[12], (L, diff_out, D_MODEL), BETA * diff_out ** -0.5),
        'w_out': nrm(ks[13], (L, D_MODEL, D_MODEL), BETA * D_MODEL ** -0.5),
        'ln1_g': gain(ks[14], (L, D_MODEL)),
        'ln1_b': nrm(ks[15], (L, D_MODEL), 0.02),
        'w_ffn_in': nrm(ks[16], (L, D_MODEL, 2 * D_FF), D_MODEL ** -0.5),
        'w_ffn_down': nrm(ks[17], (L, D_FF, D_MODEL), BETA * D_FF ** -0.5),
        'ln2_g': gain(ks[18], (L, D_MODEL)),
        'ln2_b': nrm(ks[19], (L, D_MODEL), 0.02),
    }


def reference(x, w_in, mla_q_norm, mla_w_uq, mla_kv_norm, mla_w_ukv,
              diff_lambda_q1, diff_lambda_k1, diff_lambda_q2, diff_lambda_k2, diff_subln,
              w_branch_a, w_branch_b, w_out, ln1_g, ln1_b,
              w_ffn_in, w_ffn_down, ln2_g, ln2_b):
    S = x.shape[1]
    cos_a, sin_a = _rope_tables(S, MLA_ROPE_DIM)
    cos_b, sin_b = _rope_tables(S, DIFF_QK_DIM)
    split_at = np.cumsum(IN_SIZES)[:-1].tolist()
    h = x
    for l in range(DEPTH):
        lambda_init = 0.8 - 0.6 * math.exp(-0.3 * l)
        proj = h @ w_in[l]
        c_q, c_kv, k_rope, dq, dk, dv, g_a, g_b = jnp.split(proj, split_at, axis=-1)
        y_a = _mla_branch(c_q, c_kv, k_rope, mla_q_norm[l], mla_w_uq[l],
                          mla_kv_norm[l], mla_w_ukv[l], cos_a, sin_a) @ w_branch_a[l]
        y_b = _diff_branch(dq, dk, dv, diff_lambda_q1[l], diff_lambda_k1[l],
                           diff_lambda_q2[l], diff_lambda_k2[l], diff_subln[l],
                           lambda_init, cos_b, sin_b) @ w_branch_b[l]
        mixed = (jax.nn.sigmoid(g_a) * y_a + jax.nn.sigmoid(g_b) * y_b) @ w_out[l]
        h = _layer_norm(ALPHA * h + mixed, ln1_g[l], ln1_b[l])
        gate, up = jnp.split(h @ w_ffn_in[l], 2, axis=-1)
        ffn = (jax.nn.silu(gate) * up) @ w_ffn_down[l]
        h = _layer_norm(ALPHA * h + ffn, ln2_g[l], ln2_b[l])
    return h
```

```python
from contextlib import ExitStack

import numpy as np
import concourse.bass as bass
import concourse.mybir as mybir
from concourse.bass_utils import run_bass_kernel_spmd

F32 = mybir.dt.float32
BF16 = mybir.dt.bfloat16
AF = mybir.ActivationFunctionType
ALU = mybir.AluOpType
AX = mybir.AxisListType

D = 2048
SEQ = 4096
NT = 1024
DFF = 5632
ALPHA = 2.0 ** 0.25
LAMBDA_INIT = 0.2
MLA_SCALE = 192.0 ** -0.5
DIFF_SCALE = 64.0 ** -0.5
RMS_EPS = 1e-6
SUBLN_EPS = 1e-5
LN_EPS = 1e-5
NEG = -30000.0
WSLOT = 4096
NWS = 6

ENGINES = ("pe", "act", "dve", "pool", "sp")


class Tile:
    def __init__(self, name, global_=False):
        self.name = name
        self.writers = []
        self.readers = []
        self.gen_deps = []
        self.sem = None
        self.count = 0
        self.global_ = global_


class Op:
    __slots__ = ("eng", "emit", "deps", "signal", "idx", "sigval", "is_dma", "tile", "semval", "inc", "nowait")

    def __init__(self, eng, emit):
        self.eng = eng
        self.emit = emit
        self.deps = []
        self.signal = False
        self.idx = -1
        self.sigval = 0
        self.is_dma = False
        self.tile = None
        self.semval = 0
        self.inc = 16
        self.nowait = False


class FW:
    def __init__(self, nc, es):
        self.nc = nc
        self.es = es
        self.ops = {e: [] for e in ENGINES}
        self.engsem = {e: es.enter_context(nc.semaphore("sem_" + e)) for e in ("pe", "act", "dve", "pool")}
        self.pending_dma = []
        self.barrier_deps = []
        self.need_barrier = {e: False for e in ENGINES}
        self.nsem = 4

    def _dedupe(self, deps, op):
        best = {}
        for d in deps:
            if d is None or d is op:
                continue
            if d.is_dma:
                key = ("t", id(d.tile))
                if key not in best or best[key].semval < d.semval:
                    best[key] = d
            else:
                if d.eng == "pe" and op.eng == "pe":
                    continue
                key = ("e", d.eng)
                if key not in best or best[key].idx < d.idx:
                    best[key] = d
        return list(best.values())

    def _add(self, op, reads, writes, partial, extra):
        deps = list(extra)
        if self.need_barrier[op.eng] and not (op.eng == "pool" and op.is_dma):
            deps += self.barrier_deps
            self.need_barrier[op.eng] = False
        for t in reads:
            deps += t.writers
        for t in writes:
            deps += t.readers
            deps += t.gen_deps
            if not partial:
                deps += t.writers
        op.deps = self._dedupe(deps, op)
        for d in op.deps:
            if not d.is_dma:
                d.signal = True
        for t in reads:
            if t not in writes:
                t.readers.append(op)
        for t in writes:
            if t.readers:
                t.gen_deps = [r for r in t.readers if r is not op]
                t.writers = [op]
                t.readers = []
            elif partial:
                t.writers.append(op)
            else:
                t.gen_deps = []
                t.writers = [op]
        op.idx = len(self.ops[op.eng])
        self.ops[op.eng].append(op)
        return op

    def op(self, eng, emit, reads=(), writes=(), partial=False, extra=()):
        return self._add(Op(eng, emit), list(reads), list(writes), partial, extra)

    def dma(self, queue, emit, dst, src=None, partial=True, extra=(), inc=16, after_barrier=False, sem_on_src=False,
            sem_tile=None):
        op = Op(queue, emit)
        op.is_dma = True
        st = sem_tile if sem_tile is not None else (src if sem_on_src else dst)
        op.tile = st
        op.inc = inc
        if st.sem is None:
            st.sem = self.es.enter_context(self.nc.semaphore("ts_" + st.name))
            self.nsem += 1
        st.count += inc
        op.semval = st.count
        ex = list(extra)
        if after_barrier:
            ex += self.barrier_deps
        self._add(op, [src] if src is not None else [], [dst], partial, ex)
        if not dst.global_:
            self.pending_dma.append(op)
        return op

    def barrier(self):
        deps = []
        for e in ("pe", "act", "dve", "pool"):
            last = None
            for o in reversed(self.ops[e]):
                if not o.is_dma and o.emit is not None:
                    last = o
                    break
            if last is not None:
                last.signal = True
                deps.append(last)
        deps += self.pending_dma
        self.pending_dma = []
        self.barrier_deps = deps
        for e in ("pe", "act", "dve", "sp", "pool"):
            self.need_barrier[e] = True

    def final_wait(self, eng, deps):
        op = Op(eng, None)
        op.deps = self._dedupe(list(deps), op)
        for d in op.deps:
            if not d.is_dma:
                d.signal = True
        op.idx = len(self.ops[eng])
        self.ops[eng].append(op)

    def finalize(self):
        for e in ENGINES:
            c = 0
            for op in self.ops[e]:
                if op.signal and not op.is_dma:
                    c += 1
                    op.sigval = c

    def emit(self, eng, handle):
        waited = {}
        for op in self.ops[eng]:
            for d in op.deps:
                if d.is_dma:
                    sem, val = d.tile.sem, d.semval
                else:
                    sem, val = self.engsem[d.eng], d.sigval
                k = id(sem)
                if waited.get(k, 0) >= val:
                    continue
                waited[k] = val
                handle.wait_ge(sem, val)
            if op.emit is None:
                continue
            ins = op.emit(handle)
            if op.is_dma:
                if op.inc == 16:
                    ins.then_inc(op.tile.sem, 16)
                else:
                    ins.then_inc(op.tile.sem)
            elif op.signal:
                ins.then_inc(self.engsem[eng], 1)


class Arena:
    def __init__(self, nc, nbytes):
        self.nbytes = nbytes
        self.t = nc.alloc_sbuf_tensor("arena", [128, nbytes // 4], F32)
        self.ap = self.t.ap()
        self.items = []

    def alloc(self, nbytes, p0, p1):
        nbytes = (nbytes + 63) // 64 * 64
        conf = sorted([(o, n) for (o, n, a, b) in self.items if not (b < p0 or a > p1)])
        off = 0
        for (o, n) in conf:
            if off + nbytes <= o:
                break
            off = max(off, o + n)
        assert off + nbytes <= self.nbytes, ("arena overflow", off, nbytes, p0, p1)
        self.items.append((off, nbytes, p0, p1))
        return off

    def f32(self, n, p0, p1):
        off = self.alloc(4 * n, p0, p1)
        return self.ap[:, off // 4: off // 4 + n]

    def bf16(self, n, p0, p1):
        off = self.alloc(2 * n, p0, p1)
        return self.ap[:, off // 4: off // 4 + n // 2].bitcast(BF16)


class _Stop(Exception):
    pass


def build_program(debug=False, stop_after=99):
    nc = bass.Bass("TRN2", target_bir_lowering=False)
    es = ExitStack()
    with es:
        def din(name, shape, dt=F32):
            return nc.dram_tensor(name, list(shape), dt, kind="ExternalInput").ap()

        xT_d = din("xT", [D, NT])
        xres_d = din("xres", [NT, D])
        w_in_d = din("w_in", [D, 8256])
        w_uq_d = din("w_uq", [512, 1536])
        w_ukv_d = din("w_ukv", [512, 2048])
        w_a_d = din("w_a", [1024, D])
        w_b_d = din("w_b", [1024, D])
        w_out_d = din("w_out", [D, D])
        w_fi_d = din("w_fi", [D, 2 * DFF])
        w_fd_d = din("w_fd", [DFF, D])
        cs_d = din("cs", [128, 2 * NT])
        cf_d = din("cf", [128, 144])
        cb_d = din("cb", [128, 640])
        lam_d = din("lam", [1, 256])
        ln_d = din("ln", [4, D])
        out_d = nc.dram_tensor("out", [NT, D], F32, kind="ExternalOutput").ap()
        GN = {"ckv": 512, "kpe": 128, "dk0": 512, "dk1": 512, "dv0": 512, "dv1": 512}
        gin = {k: nc.dram_tensor("gin_" + k, [n, NT], BF16) for k, n in GN.items()}
        gout = {k: nc.dram_tensor("gout_" + k, [4 * n, NT], BF16) for k, n in GN.items()}

        ar = Arena(nc, 207 * 1024)
        fw = FW(nc, es)

        banks = []
        bank_t = []
        for i in range(8):
            p = es.enter_context(nc.psum_tensor("ps%d" % i, [128, 512], F32))
            banks.append(p[:])
            bank_t.append(Tile("ps%d" % i))
        bank_rr = [0]

        def next_bank(choices=None):
            ch = choices if choices is not None else list(range(8))
            b = ch[bank_rr[0] % len(ch)]
            bank_rr[0] += 1
            return b

        def MM(out, lhsT, rhs, start, stop, reads, wt, partial=False):
            return fw.op("pe", lambda e: e.matmul(out, lhsT, rhs, start=start, stop=stop),
                         reads=reads, writes=[wt], partial=partial)

        def TR(out, in_, reads, wt, partial=False):
            return fw.op("pe", lambda e: e.transpose(out, in_, ident), reads=reads + [T_constb], writes=[wt],
                         partial=partial)

        def ACT(out, in_, func, reads, writes, scale=1.0, bias=0.0, partial=False):
            return fw.op("act", lambda e: e.activation(out=out, in_=in_, func=func, bias=bias, scale=scale),
                         reads=reads, writes=writes, partial=partial)

        def ACOPY(out, in_, reads, writes, partial=False):
            return fw.op("act", lambda e: e.copy(out=out, in_=in_), reads=reads, writes=writes, partial=partial)

        def VCOPY(out, in_, reads, writes, partial=False):
            return fw.op("dve", lambda e: e.tensor_copy(out=out, in_=in_), reads=reads, writes=writes, partial=partial)

        def VTT(out, in0, in1, op, reads, writes, partial=False):
            return fw.op("dve", lambda e: e.tensor_tensor(out=out, in0=in0, in1=in1, op=op),
                         reads=reads, writes=writes, partial=partial)

        def PTT(out, in0, in1, op, reads, writes, partial=False):
            return fw.op("pool", lambda e: e.tensor_tensor(out=out, in0=in0, in1=in1, op=op),
                         reads=reads, writes=writes, partial=partial)

        def VTS(out, in0, s1, s2, op0, op1, reads, writes, partial=False):
            if op1 is None:
                return fw.op("dve", lambda e: e.tensor_scalar(out=out, in0=in0, scalar1=s1, scalar2=None, op0=op0),
                             reads=reads, writes=writes, partial=partial)
            return fw.op("dve", lambda e: e.tensor_scalar(out=out, in0=in0, scalar1=s1, scalar2=s2, op0=op0, op1=op1),
                         reads=reads, writes=writes, partial=partial)

        def VSTT(out, in0, scalar, in1, op0, op1, reads, writes, partial=False):
            return fw.op("dve", lambda e: e.scalar_tensor_tensor(out=out, in0=in0, scalar=scalar, in1=in1,
                                                                 op0=op0, op1=op1),
                         reads=reads, writes=writes, partial=partial)

        def VRECIP(out, in_, reads, writes):
            return fw.op("dve", lambda e: e.reciprocal(out=out, in_=in_), reads=reads, writes=writes)

        def DMA(queue, out, in_, dst, src=None, **kw):
            return fw.dma(queue, lambda e: e.dma_start(out=out, in_=in_), dst, src=src, **kw)

        dbg_out = {}

        def dump(name, ap, tile, shape, dt):
            if not debug:
                return
            d = nc.dram_tensor("dbg_" + name, list(shape), dt, kind="ExternalOutput").ap()
            tt = Tile("dbg_" + name)
            dbg_out[name] = DMA("sp", d, ap, tt, src=tile)

        def phase_end(n):
            if stop_after == n:
                raise _Stop()

        cf = ar.f32(144, 0, 99)
        cb = ar.bf16(640, 0, 99)
        ones_f = ar.f32(128, 0, 99)
        ones_b = ar.bf16(128, 0, 99)
        sm = ar.f32(16, 0, 99)
        T_const = Tile("consts")
        T_constb = Tile("constsb")
        T_ones = Tile("ones")
        T_sm = Tile("small")
        RT = cf[:, 0:128]
        ident = cb[:, 0:128]
        wsl_ap = [ar.bf16(WSLOT, 0, 99) for _ in range(NWS)]
        wsl_t = [Tile("ws%d" % i, global_=True) for i in range(NWS)]
        ws_rr = [0]

        def ws_acquire():
            i = ws_rr[0] % NWS
            ws_rr[0] += 1
            return i

        def ws_load(slot, eoff, src3, K, C, first):
            dst = wsl_ap[slot][:, eoff:eoff + K * C].rearrange("p (k c) -> p k c", k=K)
            DMA("pool", dst, src3, wsl_t[slot], partial=not first)
            for it in list(ag_pending):
                it[0] -= 1
                if it[0] <= 0:
                    ag_pending.remove(it)
                    it[1]()
            return dst

        ag_pending = []

        def flush_ag(n=None):
            for it in list(ag_pending)[:n]:
                ag_pending.remove(it)
                it[1]()

        def wsrc(w_d, r0, nk, c0, C):
            return w_d[r0:r0 + nk * 128, c0:c0 + C].rearrange("(k p) c -> p k c", p=128)

        def load_w(src3, K, C):
            s = ws_acquire()
            return ws_load(s, 0, src3, K, C, True), s

        xtmp = ar.f32(8 * NT, 0, 3).rearrange("p (k t) -> p k t", k=8)
        kvbuf = ar.bf16(2 * 16384, 5, 6)
        h1 = ar.f32(8 * D, 7, 10).rearrange("p (t n) -> p t n", t=8)
        h1T = ar.bf16(16 * NT, 7, 9).rearrange("p (k t) -> p k t", k=16)
        actT = ar.bf16(11 * NT, 8, 9).rearrange("p (c t) -> p c t", c=11)
        mergedT = ar.bf16(16 * NT, 6, 7).rearrange("p (k t) -> p k t", k=16)
        dqT = ar.bf16(8 * NT, 1, 5).rearrange("p (c t) -> p c t", c=8)
        mlaT = ar.bf16(8 * NT, 4, 6).rearrange("p (c t) -> p c t", c=8)
        diffT = ar.bf16(8 * NT, 5, 6).rearrange("p (c t) -> p c t", c=8)
        qnT = ar.bf16(8 * NT, 1, 4).rearrange("p (c t) -> p c t", c=8)
        qpeT = ar.bf16(4 * NT, 1, 4).rearrange("p (c t) -> p c t", c=4)
        sq = ar.f32(2 * 512, 1, 5).rearrange("p (c n) -> p c n", c=2)
        rstd = ar.f32(512, 1, 5)
        Pb = ar.bf16(6 * 512, 4, 5).rearrange("p (c n) -> p c n", c=6)
        rec = ar.f32(2 * 512, 4, 5).rearrange("p (c n) -> p c n", c=2)

        try:
            lamraw = ar.f32(256, 0, 3)
            cs = ar.f32(2 * NT, 0, 3)
            cosT = cs[:, 0:NT]
            sinT = cs[:, NT:2 * NT]
            xT = ar.bf16(16 * NT, 0, 3).rearrange("p (k t) -> p k t", k=16)
            xT_t = [Tile("xT%d" % i) for i in range(4)]
            T_cs = Tile("cs")
            T_lam = Tile("lamraw")

            DMA("sp", cf, cf_d[:, :], T_const)
            DMA("pool", cb, cb_d[:, :], T_constb)
            DMA("sp", lamraw.rearrange("p (o n) -> p o n", o=1), lam_d[0:1, :].partition_broadcast(128), T_lam)
            DMA("sp", cs, cs_d[:, :], T_cs)
            xtmp_t = [Tile("xtmp%d" % i) for i in range(2)]
            for i in range(2):
                DMA("pool", xT[:, 4 * i:4 * i + 4, :],
                    xT_d[512 * i:512 * (i + 1), :].rearrange("(k p) t -> p k t", p=128), xT_t[i])
            for i in range(2):
                DMA("sp", xtmp[:, 4 * i:4 * i + 4, :],
                    xT_d[1024 + 512 * i:1024 + 512 * (i + 1), :].rearrange("(k p) t -> p k t", p=128), xtmp_t[i])
            VCOPY(xT[:, 8:12, :], xtmp[:, 0:4, :], [xtmp_t[0]], [xT_t[2]])
            ACOPY(xT[:, 12:16, :], xtmp[:, 4:8, :], [xtmp_t[1]], [xT_t[3]])
            fw.op("dve", lambda e: e.memset(ones_f, 1.0), writes=[T_ones])
            fw.op("dve", lambda e: e.memset(ones_b, 1.0), writes=[T_ones], partial=True)
            lr = lamraw.rearrange("p (a n) -> p a n", a=4)
            VTT(lr[:, 0, :], lr[:, 0, :], lr[:, 1, :], ALU.mult, [T_lam], [T_lam])
            VTT(lr[:, 2, :], lr[:, 2, :], lr[:, 3, :], ALU.mult, [T_lam], [T_lam])
            fw.op("dve", lambda e: e.reduce_sum(out=sm[:, 2:3], in_=lr[:, 0, :], axis=AX.X), reads=[T_lam], writes=[T_sm])
            fw.op("dve", lambda e: e.reduce_sum(out=sm[:, 3:4], in_=lr[:, 2, :], axis=AX.X), reads=[T_lam, T_sm],
                  writes=[T_sm])
            ACT(sm[:, 4:6], sm[:, 2:4], AF.Exp, [T_sm], [T_sm])
            VTT(sm[:, 6:7], sm[:, 5:6], sm[:, 4:5], ALU.subtract, [T_sm], [T_sm])
            VTS(sm[:, 0:1], sm[:, 6:7], -LAMBDA_INIT, None, ALU.add, None, [T_sm], [T_sm])
            VTS(sm[:, 1:2], cf[:, 136:137], 1.0 - LAMBDA_INIT, None, ALU.mult, None, [T_sm, T_const], [T_sm])
            fw.op("dve", lambda e: e.memset(sm[:, 8:9], RMS_EPS), writes=[T_sm], reads=[T_sm])
            fw.op("dve", lambda e: e.memset(sm[:, 9:10], SUBLN_EPS), writes=[T_sm], reads=[T_sm])
            fw.op("dve", lambda e: e.memset(sm[:, 10:11], LN_EPS), writes=[T_sm], reads=[T_sm])
            eps_rms, eps_sub, eps_ln = sm[:, 8:9], sm[:, 9:10], sm[:, 10:11]
            neglam = sm[:, 0:1]
            gsub = sm[:, 1:2]

            phase_end(0)
            rawf = ar.f32(4 * 512, 1, 3).rearrange("p (c n) -> p c n", c=4)
            rawf_t = [Tile("rawf%d" % i) for i in range(4)]
            sq_t = [Tile("sq%d" % i) for i in range(2)]
            rstd_t = Tile("rstd")
            xf = ar.f32(2 * 512, 1, 3).rearrange("p (c n) -> p c n", c=2)
            xf_t = [Tile("xf%d" % i) for i in range(2)]
            t1 = ar.f32(2 * 512, 1, 3).rearrange("p (c n) -> p c n", c=2)
            t1_t = [Tile("t1%d" % i) for i in range(2)]
            t2 = ar.f32(2 * 512, 1, 3).rearrange("p (c n) -> p c n", c=2)
            t2_t = [Tile("t2%d" % i) for i in range(2)]
            stg = ar.bf16(4 * 512, 1, 3).rearrange("p (c n) -> p c n", c=4)
            stg_t = [Tile("stg%d" % i) for i in range(4)]
            rr = {"sq": 0, "xf": 0, "stg": 0}

            pe_defer = []

            def defer(fn):
                pe_defer.append(fn)

            def flush_deferred():
                pend = pe_defer[:]
                pe_defer.clear()
                for f in pend:
                    f()

            def flush_all():
                while pe_defer:
                    flush_deferred()

            def chain(bank, n_out, nk, lhsT_fn, rhs_fn, reads_fn, col0=0, cont=False):
                pend = pe_defer[:]
                pe_defer.clear()
                for k in range(nk):
                    MM(banks[bank][:, col0:col0 + n_out], lhsT_fn(k), rhs_fn(k), k == 0, k == nk - 1,
                       reads_fn(k), bank_t[bank], partial=(k > 0 or cont))
                for f in pend:
                    f()

            def rope_to(bank, g, dst, dst_t, after=None):
                i = rr["xf"] % 2
                rr["xf"] += 1
                ACOPY(xf[:, i, :], banks[bank][:, 0:512], [bank_t[bank]], [xf_t[i]])

                def part2():
                    b2 = next_bank()
                    MM(banks[b2][:, 0:512], RT, xf[:, i, :], True, True, [T_const, xf_t[i]], bank_t[b2])
                    VTT(t1[:, i, :], xf[:, i, :], cosT[:, g * 512:(g + 1) * 512], ALU.mult, [xf_t[i], T_cs], [t1_t[i]])
                    VTT(t2[:, i, :], banks[b2][:, 0:512], sinT[:, g * 512:(g + 1) * 512], ALU.mult,
                        [bank_t[b2], T_cs], [t2_t[i]])
                    VTT(dst, t1[:, i, :], t2[:, i, :], ALU.add, [t1_t[i], t2_t[i]], [dst_t], partial=True)
                    if after is not None:
                        after()
                defer(part2)

            def latent_norm(wa, sa_, wb, sb_, gain_col0, g, src, src_t, dst_fn, after=None):
                sbk = next_bank()

                def stat_mm(i, c):
                    MM(banks[sbk][:, 0:512], ones_f, sq[:, i, :], c == 0, c == 3, [T_ones, sq_t[i]], bank_t[sbk],
                       partial=(c > 0))

                for c in range(4):
                    w_, s_ = (wa, sa_) if c < 2 else (wb, sb_)
                    cc = c % 2
                    b = next_bank()
                    chain(b, 512, 16, lambda k: w_[:, k, cc * 128:(cc + 1) * 128],
                          lambda k: src[:, k, g * 512:(g + 1) * 512], lambda k: [src_t[k // 4], wsl_t[s_]])
                    ACOPY(rawf[:, c, :], banks[b][:, 0:512], [bank_t[b]], [rawf_t[c]])
                    i = rr["sq"] % 2
                    rr["sq"] += 1
                    ACT(sq[:, i, :], banks[b][:, 0:512], AF.Square, [bank_t[b]], [sq_t[i]])
                    defer(lambda i=i, c=c: stat_mm(i, c))

                def final():
                    ACT(rstd, banks[sbk][:, 0:512], AF.Ln, [bank_t[sbk], T_sm], [rstd_t], scale=1.0 / 512.0,
                        bias=eps_rms)
                    ACT(rstd, rstd, AF.Exp, [rstd_t], [rstd_t], scale=-0.5)
                    for c in range(4):
                        dst, dt_ = dst_fn(c)
                        VSTT(dst, rawf[:, c, :], cf[:, gain_col0 + c:gain_col0 + c + 1], rstd, ALU.mult, ALU.mult,
                             [rawf_t[c], rstd_t, T_const], [dt_], partial=True)
                    if after is not None:
                        after()
                defer(final)

            def new_stg():
                i = rr["stg"] % 4
                rr["stg"] += 1
                return i

            T_gin = {k: Tile("gin_" + k) for k in GN}
            T_gout = {k: Tile("gout_" + k) for k in GN}

            def all_gather(k):
                def issue():
                    fw.dma("pool", lambda e: e.collective_compute(
                        "AllGather", ALU.bypass, replica_groups=RG, ins=[gin[k].ap().opt()],
                        outs=[gout[k].ap().opt()]), T_gout[k], src=T_gin[k], partial=False, inc=1)
                ag_pending.append([10 ** 9, issue])
            RG = [[0, 1, 2, 3], [4, 5, 6, 7]]

            w_ckv, s_ckv = load_w(wsrc(w_in_d, 0, 16, 512, 256), 16, 256)
            w_ckv2, s_ckv2 = load_w(wsrc(w_in_d, 0, 16, 768, 256), 16, 256)
            def ckv_group(g):
                used = []

                def dst_fn(c):
                    i = new_stg()
                    used.append((c, i))
                    return stg[:, i, :], stg_t[i]

                def after():
                    for (c, i) in used:
                        DMA("sp", gin["ckv"].ap()[c * 128:(c + 1) * 128, g * 512:(g + 1) * 512], stg[:, i, :],
                            T_gin["ckv"], src=stg_t[i], sem_on_src=True)
                latent_norm(w_ckv, s_ckv, w_ckv2, s_ckv2, 132, g, xT, xT_t, dst_fn, after)
            for g in range(2):
                ckv_group(g)
            phase_end(0.1)
            s_kr = ws_acquire()
            w_kr = wsl_ap[s_kr][:, 0:16 * 128].rearrange("p (k c) -> p k c", k=16)
            for h2 in range(2):
                DMA("pool", w_kr[:, :, h2 * 64:(h2 + 1) * 64], wsrc(w_in_d, 0, 16, 1024, 64), wsl_t[s_kr],
                    partial=(h2 > 0))
            def roped_store(b, g, key, r0):
                ii = new_stg()

                def dst_after():
                    DMA("sp", gin[key].ap()[r0:r0 + 128, g * 512:(g + 1) * 512], stg[:, ii, :], T_gin[key],
                        src=stg_t[ii], sem_on_src=True)
                rope_to(b, g, stg[:, ii, :], stg_t[ii], after=dst_after)

            for g in range(2):
                b = next_bank()
                chain(b, 512, 16, lambda k: w_kr[:, k, :], lambda k: xT[:, k, g * 512:(g + 1) * 512],
                      lambda k: [xT_t[k // 4], wsl_t[s_kr]])
                roped_store(b, g, "kpe", 0)
            flush_all()
            all_gather("ckv")
            phase_end(0.2)
            all_gather("kpe")
            phase_end(0.3)
            for cg in range(4):
                w_dk, s_dk = load_w(wsrc(w_in_d, 0, 16, 2112 + cg * 256, 256), 16, 256)
                for cc in range(2):
                    c = cg * 2 + cc
                    for g in range(2):
                        b = next_bank()
                        chain(b, 512, 16, lambda k: w_dk[:, k, cc * 128:(cc + 1) * 128],
                              lambda k: xT[:, k, g * 512:(g + 1) * 512], lambda k: [xT_t[k // 4], wsl_t[s_dk]])
                        roped_store(b, g, "dk%d" % (c // 4), (c % 4) * 128)
                if cg % 2 == 1:
                    flush_all()
                    all_gather("dk%d" % (cg // 2))
            phase_end(0.4)
            for cg in range(4):
                w_dv, s_dv = load_w(wsrc(w_in_d, 0, 16, 3136 + cg * 256, 256), 16, 256)
                for tp in range(4):
                    b = next_bank()
                    for hh in range(2):
                        t = tp * 2 + hh
                        chain(b, 256, 16, lambda k: xT[:, k, t * 128:(t + 1) * 128], lambda k: w_dv[:, k, :],
                              lambda k: [xT_t[k // 4], wsl_t[s_dv]], col0=hh * 256, cont=(hh > 0))
                    i = new_stg()
                    ACOPY(stg[:, i, :], banks[b][:, 0:512], [bank_t[b]], [stg_t[i]])
                    for hh in range(2):
                        t = tp * 2 + hh
                        kk_ = "dv%d" % (t // 4)
                        DMA("sp", gin[kk_].ap()[(t % 4) * 128:(t % 4 + 1) * 128, cg * 256:(cg + 1) * 256],
                            stg[:, i, hh * 256:(hh + 1) * 256], T_gin[kk_], src=stg_t[i], sem_on_src=True)
            phase_end(0.5)
            all_gather("dv0")
            all_gather("dv1")

            phase_end(1)
            cqnT = ar.bf16(4 * NT, 1, 3).rearrange("p (c t) -> p c t", c=4)
            cqn_t = [Tile("cqnT")]
            qn_t = [Tile("qnT%d" % h) for h in range(8)]
            qpe_t = [Tile("qpeT%d" % j) for j in range(4)]
            dq_t = [Tile("dqT%d" % h) for h in range(8)]

            w_cq, s_cq = load_w(wsrc(w_in_d, 0, 16, 0, 256), 16, 256)
            w_cq2, s_cq2 = load_w(wsrc(w_in_d, 0, 16, 256, 256), 16, 256)
            for g in range(2):
                latent_norm(w_cq, s_cq, w_cq2, s_cq2, 128, g, xT, xT_t,
                            lambda c, g=g: (cqnT[:, c, g * 512:(g + 1) * 512], cqn_t[0]))
            flush_all()
            w_uqn, s_uqn = load_w(wsrc(w_uq_d, 0, 4, 0, 1024), 4, 1024)
            w_uqr, s_uqr = load_w(wsrc(w_uq_d, 0, 4, 1024, 512), 4, 512)
            for h in range(8):
                for g in range(2):
                    b = next_bank()
                    chain(b, 512, 4, lambda k: w_uqn[:, k, h * 128:(h + 1) * 128],
                          lambda k: cqnT[:, k, g * 512:(g + 1) * 512], lambda k: [cqn_t[0], wsl_t[s_uqn]])
                    ACOPY(qnT[:, h, g * 512:(g + 1) * 512], banks[b][:, 0:512], [bank_t[b]], [qn_t[h]], partial=True)
            for j in range(4):
                for g in range(2):
                    b = next_bank()
                    chain(b, 512, 4, lambda k: w_uqr[:, k, j * 128:(j + 1) * 128],
                          lambda k: cqnT[:, k, g * 512:(g + 1) * 512], lambda k: [cqn_t[0], wsl_t[s_uqr]])
                    rope_to(b, g, qpeT[:, j, g * 512:(g + 1) * 512], qpe_t[j])

            phase_end(2)
            for cg in range(4):
                w_dq, s_dq = load_w(wsrc(w_in_d, 0, 16, 1088 + cg * 256, 256), 16, 256)
                if cg == 1:
                    flush_ag(4)
                if cg == 3:
                    flush_ag()
                for cc in range(2):
                    h = cg * 2 + cc
                    for g in range(2):
                        b = next_bank()
                        chain(b, 512, 16, lambda k: w_dq[:, k, cc * 128:(cc + 1) * 128],
                              lambda k: xT[:, k, g * 512:(g + 1) * 512], lambda k: [xT_t[k // 4], wsl_t[s_dq]])
                        rope_to(b, g, dqT[:, h, g * 512:(g + 1) * 512], dq_t[h])

            flush_all()
            flush_ag()
            phase_end(3)
            fw.barrier()

            ckv_all = ar.bf16(4 * SEQ, 4, 4).rearrange("p (c t) -> p c t", c=4)
            T_ckv_all = Tile("ckv_all")
            kpe_all = ar.bf16(SEQ, 4, 4)
            knT = ar.bf16(2 * SEQ, 4, 4).rearrange("p (c t) -> p c t", c=2)
            kn_t = [Tile("knT%d" % i) for i in range(2)]
            Vp = ar.bf16(32 * 256, 4, 4).rearrange("p (t c) -> p t c", t=32)
            Vp_t = Tile("Vp")
            P_t = [Tile("P%d" % i) for i in range(6)]
            rec_t = [Tile("rec%d" % i) for i in range(2)]
            mla_t = [Tile("mlaT%d" % h) for h in range(8)]
            diff_t = [Tile("diffT%d" % h) for h in range(8)]

            for r_ in range(4):
                DMA("sp", ckv_all[:, :, r_ * NT:(r_ + 1) * NT],
                    gout["ckv"].ap()[r_ * 512:(r_ + 1) * 512, :].rearrange("(c p) t -> p c t", p=128),
                    T_ckv_all, src=T_gout["ckv"])
                DMA("sp", kpe_all[:, r_ * NT:(r_ + 1) * NT], gout["kpe"].ap()[r_ * 128:(r_ + 1) * 128, :],
                    T_ckv_all, src=T_gout["kpe"])
            w_uk, s_uk = load_w(wsrc(w_ukv_d, 0, 4, 0, 1024), 4, 1024)
            w_uv, s_uv = load_w(wsrc(w_ukv_d, 0, 4, 1024, 1024), 4, 1024)

            def steps_for(G):
                st = []
                for r_ in range(4):
                    for j_ in range(4 * G + 4):
                        a = j_ - 4 * G
                        st.append((r_, j_, 128 * a if a >= 0 else 0, a >= 0))
                return st

            S_BANKS_MLA = [0, 1, 2]
            O_BANK, SUM_BANK = 3, 4
            PROD_BANKS = [5, 6, 7]
            p_rr = [0]
            rec_rr = [0]

            def mla_attention(h, hh, ks, G):
                j, eh = h // 2, h % 2
                steps = steps_for(G)
                ns = len(steps)

                def emit_S(s):
                    r_, j_, col0, dg = steps[s]
                    n = 512 - col0
                    tok = r_ * NT + j_ * 128
                    sbk = S_BANKS_MLA[s % 3]
                    q0 = G * 512 + col0
                    MM(banks[sbk][:, 0:n], knT[:, ks, tok:tok + 128], qnT[:, h, q0:q0 + n], True, False,
                       [kn_t[ks], qn_t[h]], bank_t[sbk])
                    MM(banks[sbk][:, 0:n], kpe_all[eh * 64:(eh + 1) * 64, tok:tok + 128],
                       qpeT[eh * 64:(eh + 1) * 64, j, q0:q0 + n], False, not dg,
                       [T_ckv_all, qpe_t[j]], bank_t[sbk], partial=True)
                    if dg:
                        MM(banks[sbk][:, 0:128], ident, cb[:, 128 + 128 * r_:256 + 128 * r_], False, True,
                           [T_constb], bank_t[sbk], partial=True)

                emit_S(0)
                if ns > 1:
                    emit_S(1)
                for s in range(ns):
                    r_, j_, col0, dg = steps[s]
                    n = 512 - col0
                    sbk = S_BANKS_MLA[s % 3]
                    pi = p_rr[0] % 6
                    p_rr[0] += 1
                    ACT(Pb[:, pi, 0:n], banks[sbk][:, 0:n], AF.Exp, [bank_t[sbk]], [P_t[pi]], scale=MLA_SCALE)
                    if s + 2 < ns:
                        emit_S(s + 2)
                    tt = r_ * 8 + j_
                    MM(banks[O_BANK][:, col0:512], Vp[:, tt, hh * 128:(hh + 1) * 128], Pb[:, pi, 0:n], s == 0, s == ns - 1,
                       [Vp_t, P_t[pi]], bank_t[O_BANK], partial=(s > 0))
                    MM(banks[SUM_BANK][:, col0:512], ones_b, Pb[:, pi, 0:n], s == 0, s == ns - 1,
                       [T_ones, P_t[pi]], bank_t[SUM_BANK], partial=(s > 0))
                ri = rec_rr[0] % 2
                rec_rr[0] += 1
                ACT(rec[:, ri, :], banks[SUM_BANK][:, 0:512], AF.Ln, [bank_t[SUM_BANK]], [rec_t[ri]])
                ACT(rec[:, ri, :], rec[:, ri, :], AF.Exp, [rec_t[ri]], [rec_t[ri]], scale=-1.0)
                VTT(mlaT[:, h, G * 512:(G + 1) * 512], banks[O_BANK][:, 0:512], rec[:, ri, :], ALU.mult,
                    [bank_t[O_BANK], rec_t[ri]], [mla_t[h]], partial=True)

            for hp in range(4):
                for tp in range(16):
                    b = next_bank(PROD_BANKS)
                    for hh in range(2):
                        tt = tp * 2 + hh
                        chain(b, 256, 4, lambda k: ckv_all[:, k, tt * 128:(tt + 1) * 128],
                              lambda k: w_uv[:, k, hp * 256:(hp + 1) * 256], lambda k: [T_ckv_all, wsl_t[s_uv]],
                              col0=hh * 256, cont=(hh > 0))
                    VCOPY(Vp[:, 2 * tp:2 * tp + 2, :], banks[b][:, 0:512].rearrange("p (t c) -> p t c", t=2),
                          [bank_t[b]], [Vp_t], partial=True)
                for hh in range(2):
                    h = hp * 2 + hh
                    ks = h % 2
                    for tg in range(8):
                        b = next_bank(PROD_BANKS)
                        chain(b, 512, 4, lambda k: w_uk[:, k, h * 128:(h + 1) * 128],
                              lambda k: ckv_all[:, k, tg * 512:(tg + 1) * 512], lambda k: [T_ckv_all, wsl_t[s_uk]])
                        VCOPY(knT[:, ks, tg * 512:(tg + 1) * 512], banks[b][:, 0:512], [bank_t[b]], [kn_t[ks]],
                              partial=True)
                    for G in range(2):
                        mla_attention(h, hh, ks, G)
            for h in range(8):
                dump("mlaT%d" % h, mlaT[:, h, :], mla_t[h], [128, NT], BF16)

            phase_end(4)
            fw.barrier()

            kvs = [kvbuf[:, sl_ * 16384:(sl_ + 1) * 16384] for sl_ in range(2)]
            dkT_s = [kvs[sl_][:, 0:8192].rearrange("p (c t) -> p c t", c=2) for sl_ in range(2)]
            dvp_s = [kvs[sl_][:, 8192:16384].rearrange("p (t c) -> p t c", t=32) for sl_ in range(2)]
            kv_t = [Tile("kv%d" % i) for i in range(2)]
            xT2 = kvs[0].rearrange("p (k t) -> p k t", k=16)
            T_xT2sem = Tile("xT2sem")
            dsc = ar.f32(3 * 512, 5, 5).rearrange("p (c n) -> p c n", c=3)
            dsc_t = [Tile("dsc%d" % i) for i in range(3)]
            S_BANKS_D = [0, 1, 2]
            STAT_BANK_D = 3
            O1B, O2B, S1B, S2B = 4, 5, 6, 7

            diff_stages = [{}]

            def diff_attention(h, hh, sl, G):
                steps = steps_for(G)
                ns = len(steps)
                cur_stages = diff_stages[0]
                diff_stages[0] = {}

                def emit_S(s):
                    r_, j_, col0, dg = steps[s]
                    n = 512 - col0
                    tok = r_ * NT + j_ * 128
                    q0 = G * 512 + col0
                    for m in range(2):
                        sbk = S_BANKS_D[(2 * s + m) % 3]
                        MM(banks[sbk][:, 0:n], dkT_s[sl][m * 64:(m + 1) * 64, hh, tok:tok + 128],
                           dqT[m * 64:(m + 1) * 64, h, q0:q0 + n], True, not dg, [kv_t[sl], dq_t[h]], bank_t[sbk])
                        if dg:
                            MM(banks[sbk][:, 0:128], ident, cb[:, 128 + 128 * r_:256 + 128 * r_], False, True,
                               [T_constb], bank_t[sbk], partial=True)

                emit_S(0)
                for s in range(ns):
                    r_, j_, col0, dg = steps[s]
                    n = 512 - col0
                    pis = []
                    for m in range(2):
                        sbk = S_BANKS_D[(2 * s + m) % 3]
                        pi = p_rr[0] % 6
                        p_rr[0] += 1
                        pis.append(pi)
                        ACT(Pb[:, pi, 0:n], banks[sbk][:, 0:n], AF.Exp, [bank_t[sbk]], [P_t[pi]], scale=DIFF_SCALE)
                    if s + 1 < ns:
                        emit_S(s + 1)
                    tt = r_ * 8 + j_
                    for m in range(2):
                        pi = pis[m]
                        ob, sb_ = (O1B, S1B) if m == 0 else (O2B, S2B)
                        MM(banks[ob][:, col0:512], dvp_s[sl][:, tt, hh * 128:(hh + 1) * 128], Pb[:, pi, 0:n],
                           s == 0, s == ns - 1, [kv_t[sl], P_t[pi]], bank_t[ob], partial=(s > 0))
                        MM(banks[sb_][:, col0:512], ones_b, Pb[:, pi, 0:n], s == 0, s == ns - 1,
                           [T_ones, P_t[pi]], bank_t[sb_], partial=(s > 0))
                    for f in cur_stages.pop(s, []):
                        f()
                for s_ in sorted(cur_stages):
                    for f in cur_stages[s_]:
                        f()
                VCOPY(dsc[:, 0, :], banks[O1B][:, 0:512], [bank_t[O1B]], [dsc_t[0]])
                VCOPY(dsc[:, 1, :], banks[O2B][:, 0:512], [bank_t[O2B]], [dsc_t[1]])
                VCOPY(rec[:, 0, :], banks[S1B][:, 0:512], [bank_t[S1B]], [rec_t[0]])
                VCOPY(rec[:, 1, :], banks[S2B][:, 0:512], [bank_t[S2B]], [rec_t[1]])
                stb = STAT_BANK_D

                def st_a():
                    ACT(rec[:, 0, :], rec[:, 0, :], AF.Ln, [rec_t[0]], [rec_t[0]])
                    ACT(rec[:, 1, :], rec[:, 1, :], AF.Ln, [rec_t[1]], [rec_t[1]])

                def st_b():
                    ACT(rec[:, 0, :], rec[:, 0, :], AF.Exp, [rec_t[0]], [rec_t[0]], scale=-1.0)
                    ACT(rec[:, 1, :], rec[:, 1, :], AF.Exp, [rec_t[1]], [rec_t[1]], scale=-1.0)
                    VTT(dsc[:, 0, :], dsc[:, 0, :], rec[:, 0, :], ALU.mult, [dsc_t[0], rec_t[0]], [dsc_t[0]])
                    VTT(dsc[:, 1, :], dsc[:, 1, :], rec[:, 1, :], ALU.mult, [dsc_t[1], rec_t[1]], [dsc_t[1]])
                    VSTT(dsc[:, 2, :], dsc[:, 1, :], neglam, dsc[:, 0, :], ALU.mult, ALU.add,
                         [dsc_t[0], dsc_t[1], T_sm], [dsc_t[2]])
                    VTT(sq[:, 0, :], dsc[:, 2, :], dsc[:, 2, :], ALU.mult, [dsc_t[2]], [sq_t[0]])

                def st_c():
                    MM(banks[stb][:, 0:512], ones_f, sq[:, 0, :], True, True, [T_ones, sq_t[0]], bank_t[stb])
                    ACT(rstd, banks[stb][:, 0:512], AF.Ln, [bank_t[stb], T_sm], [rstd_t], scale=1.0 / 128.0,
                        bias=eps_sub)

                def st_d():
                    ACT(rstd, rstd, AF.Exp, [rstd_t], [rstd_t], scale=-0.5)
                    VSTT(diffT[:, h, G * 512:(G + 1) * 512], dsc[:, 2, :], gsub, rstd, ALU.mult, ALU.mult,
                         [dsc_t[2], rstd_t, T_sm], [diff_t[h]], partial=True)
                diff_stages[0] = {1: [st_a], 2: [st_b], 5: [st_c], 7: [st_d]}

            for hp in range(4):
                sl = hp % 2
                for r_ in range(4):
                    kk_ = "dk%d" % (hp // 2)
                    r0_ = r_ * 512 + (hp % 2) * 256
                    DMA("sp", dkT_s[sl][:, :, r_ * NT:(r_ + 1) * NT],
                        gout[kk_].ap()[r0_:r0_ + 256, :].rearrange("(c p) t -> p c t", p=128),
                        kv_t[sl], src=T_gout[kk_])
                    for tf in range(2):
                        kv_ = "dv%d" % tf
                        DMA("sp", dvp_s[sl][:, r_ * 8 + 4 * tf:r_ * 8 + 4 * tf + 4, :],
                            gout[kv_].ap()[r_ * 512:(r_ + 1) * 512, hp * 256:(hp + 1) * 256].rearrange(
                                "(j p) c -> p j c", p=128), kv_t[sl], src=T_gout[kv_])
                if hp == 3:
                    for i in range(4):
                        DMA("pool", xT2[:, 4 * i:4 * i + 4, :],
                            xT_d[512 * i:512 * (i + 1), :].rearrange("(k p) t -> p k t", p=128), kv_t[0],
                            sem_tile=T_xT2sem)
                for hh in range(2):
                    for G in range(2):
                        diff_attention(hp * 2 + hh, hh, sl, G)
            for s_ in sorted(diff_stages[0]):
                for f in diff_stages[0][s_]:
                    f()
            flush_all()
            for h in range(8):
                dump("diffT%d" % h, diffT[:, h, :], diff_t[h], [128, NT], BF16)

            phase_end(5)
            fw.barrier()

            mg_t = [Tile("mg%d" % i) for i in range(16)]
            esc = ar.f32(8 * 512, 6, 6).rearrange("p (c n) -> p c n", c=8)
            esc_t = [Tile("esc%d" % i) for i in range(8)]
            e_rr = [0]
            for fg in range(8):
                s_ab = ws_acquire()
                w_a = ws_load(s_ab, 0, wsrc(w_a_d, 0, 8, fg * 256, 256), 8, 256, True)
                w_b = ws_load(s_ab, 2048, wsrc(w_b_d, 0, 8, fg * 256, 256), 8, 256, False)
                w_ga, s_ga = load_w(wsrc(w_in_d, 0, 16, 4160 + fg * 256, 256), 16, 256)
                w_gb, s_gb = load_w(wsrc(w_in_d, 0, 16, 6208 + fg * 256, 256), 16, 256)
                for cc in range(2):
                    c = fg * 2 + cc
                    for g in range(2):
                        bya, byb, bga, bgb = next_bank(), next_bank(), next_bank(), next_bank()
                        chain(bya, 512, 8, lambda k: w_a[:, k, cc * 128:(cc + 1) * 128],
                              lambda k: mlaT[:, k, g * 512:(g + 1) * 512], lambda k: [mla_t[k], wsl_t[s_ab]])
                        chain(byb, 512, 8, lambda k: w_b[:, k, cc * 128:(cc + 1) * 128],
                              lambda k: diffT[:, k, g * 512:(g + 1) * 512], lambda k: [diff_t[k], wsl_t[s_ab]])
                        chain(bga, 512, 16, lambda k: w_ga[:, k, cc * 128:(cc + 1) * 128],
                              lambda k: xT2[:, k, g * 512:(g + 1) * 512], lambda k: [kv_t[0], wsl_t[s_ga]])
                        chain(bgb, 512, 16, lambda k: w_gb[:, k, cc * 128:(cc + 1) * 128],
                              lambda k: xT2[:, k, g * 512:(g + 1) * 512], lambda k: [kv_t[0], wsl_t[s_gb]])
                        o = 4 * (e_rr[0] % 2)
                        e_rr[0] += 1
                        ACT(esc[:, o, :], banks[bga][:, 0:512], AF.Sigmoid, [bank_t[bga]], [esc_t[o]])
                        ACT(esc[:, o + 1, :], banks[bgb][:, 0:512], AF.Sigmoid, [bank_t[bgb]], [esc_t[o + 1]])
                        VTT(esc[:, o + 2, :], banks[bya][:, 0:512], esc[:, o, :], ALU.mult,
                            [bank_t[bya], esc_t[o]], [esc_t[o + 2]])
                        VTT(esc[:, o + 3, :], banks[byb][:, 0:512], esc[:, o + 1, :], ALU.mult,
                            [bank_t[byb], esc_t[o + 1]], [esc_t[o + 3]])
                        VTT(mergedT[:, c, g * 512:(g + 1) * 512], esc[:, o + 2, :], esc[:, o + 3, :], ALU.add,
                            [esc_t[o + 2], esc_t[o + 3]], [mg_t[c]], partial=True)
            for c in range(16):
                dump("mg%d" % c, mergedT[:, c, :], mg_t[c], [128, NT], BF16)

            phase_end(6)
            fw.barrier()

            h1_t = [Tile("h1_%d" % t) for t in range(8)]
            lngb = ar.f32(2 * D, 7, 10).rearrange("p (c n) -> p c n", c=2)
            sg = ar.f32(2 * 512, 7, 9).rearrange("p (c n) -> p c n", c=2)
            T_lngb = Tile("lngb")
            lnst = ar.f32(64, 7, 10)
            lnst_t = Tile("lnst")
            hb = ar.bf16(D, 7, 9)
            hb_t = Tile("hb")
            h1T_t = [Tile("h1T_%d" % g) for g in range(2)]
            for t in range(8):
                DMA("sp", h1[:, t, :], xres_d[t * 128:(t + 1) * 128, :], h1_t[t], after_barrier=True)
            for i in range(2):
                DMA("sp", lngb[:, i:i + 1, :], ln_d[i:i + 1, :].partition_broadcast(128), T_lngb, after_barrier=True)

            def layer_norm(t, gb, gb_t):
                for c in range(4):
                    cc = c
                    fw.op("dve", lambda e, cc=cc: e.bn_stats(out=lnst[:, 6 * cc:6 * cc + 6],
                                                             in_=h1[:, t, cc * 512:(cc + 1) * 512]),
                          reads=[h1_t[t], lnst_t] if c == 0 else [h1_t[t]], writes=[lnst_t], partial=(c > 0))
                fw.op("dve", lambda e: e.bn_aggr(out=lnst[:, 24:26], in_=lnst[:, 0:24]), reads=[lnst_t], writes=[lnst_t])
                ACT(lnst[:, 26:27], lnst[:, 25:26], AF.Ln, [lnst_t, T_sm], [lnst_t], bias=eps_ln)
                ACT(lnst[:, 26:27], lnst[:, 26:27], AF.Exp, [lnst_t], [lnst_t], scale=-0.5)
                VSTT(lnst[:, 27:28], lnst[:, 24:25], -1.0, lnst[:, 26:27], ALU.mult, ALU.mult, [lnst_t], [lnst_t])
                ACT(h1[:, t, :], h1[:, t, :], AF.Identity, [h1_t[t], lnst_t], [h1_t[t]],
                    scale=lnst[:, 26:27], bias=lnst[:, 27:28])
                VTT(h1[:, t, :], h1[:, t, :], gb[:, 0, :], ALU.mult, [h1_t[t], gb_t], [h1_t[t]])
                VTT(h1[:, t, :], h1[:, t, :], gb[:, 1, :], ALU.add, [h1_t[t], gb_t], [h1_t[t]])

            def ln1_pre(t):
                layer_norm(t, lngb, T_lngb)
                ACOPY(hb, h1[:, t, :], [h1_t[t]], [hb_t])

            def ln1_post(t):
                for kq in range(4):
                    b = next_bank()
                    pb = banks[b].bitcast(BF16)
                    for kk in range(4):
                        k = kq * 4 + kk
                        TR(pb[:, kk * 128:(kk + 1) * 128], hb[:, k * 128:(k + 1) * 128], [hb_t], bank_t[b],
                           partial=(kk > 0))
                    VCOPY(h1T[:, 4 * kq:4 * kq + 4, t * 128:(t + 1) * 128],
                          pb[:, 0:512].rearrange("p (k n) -> p k n", k=4), [bank_t[b]], [h1T_t[t // 4]], partial=True)

            for half in range(2):
                for cg in range(8):
                    w_o, s_o = load_w(wsrc(w_out_d, 0, 16, cg * 256, 256), 16, 256)
                    for t in range(4 * half, 4 * half + 4):
                        b = next_bank()
                        chain(b, 256, 16, lambda k: mergedT[:, k, t * 128:(t + 1) * 128], lambda k: w_o[:, k, :],
                              lambda k: [mg_t[k], wsl_t[s_o]])
                        VSTT(h1[:, t, cg * 256:(cg + 1) * 256], h1[:, t, cg * 256:(cg + 1) * 256], ALPHA,
                             banks[b][:, 0:256], ALU.mult, ALU.add, [bank_t[b], h1_t[t]], [h1_t[t]])
                    if half == 1 and cg % 2 == 1:
                        tt_ = cg // 2
                        if tt_ > 0:
                            ln1_post(tt_ - 1)
                        ln1_pre(tt_)
            ln1_post(3)
            phase_end(7)

            act_t = Tile("actT")
            sg_t = [Tile("sg%d" % i) for i in range(2)]
            sg_rr = [0]
            out_t = [Tile("out%d" % t) for t in range(8)]
            stores = []

            def ln2_store(t):
                layer_norm(t, lngb, T_lngb)
                stores.append(DMA("sp", out_d[t * 128:(t + 1) * 128, :], h1[:, t, :], out_t[t], src=h1_t[t]))

            def ffn_down(q, cg, ts):
                w_d, s_d = load_w(wsrc(w_fd_d, q * 1408, 11, cg * 256, 256), 11, 256)
                for t in ts:
                    b = next_bank()
                    chain(b, 256, 11, lambda k: actT[:, k, t * 128:(t + 1) * 128], lambda k: w_d[:, k, :],
                          lambda k: [act_t, wsl_t[s_d]])
                    hsl = h1[:, t, cg * 256:(cg + 1) * 256]
                    if q == 0:
                        VSTT(hsl, hsl, ALPHA, banks[b][:, 0:256], ALU.mult, ALU.add, [bank_t[b], h1_t[t]], [h1_t[t]])
                    else:
                        VTT(hsl, hsl, banks[b][:, 0:256], ALU.add, [bank_t[b], h1_t[t]], [h1_t[t]])

            def ffn_in(q, fc, g):
                fcg = q * 11 + fc
                w_f, s_f = load_w(wsrc(w_fi_d, 0, 16, fcg * 256, 256), 16, 256)
                gs = [g] if g is not None else [0, 1]
                for g_ in gs:
                    bg, bu = next_bank(), next_bank()
                    chain(bg, 512, 16, lambda k: w_f[:, k, 0:128], lambda k: h1T[:, k, g_ * 512:(g_ + 1) * 512],
                          lambda k: [h1T_t[g_], wsl_t[s_f]])
                    chain(bu, 512, 16, lambda k: w_f[:, k, 128:256], lambda k: h1T[:, k, g_ * 512:(g_ + 1) * 512],
                          lambda k: [h1T_t[g_], wsl_t[s_f]])
                    si = sg_rr[0] % 2
                    sg_rr[0] += 1
                    ACT(sg[:, si, :], banks[bg][:, 0:512], AF.Silu, [bank_t[bg]], [sg_t[si]])
                    VTT(actT[:, fc, g_ * 512:(g_ + 1) * 512], banks[bu][:, 0:512], sg[:, si, :], ALU.mult,
                        [bank_t[bu], sg_t[si]], [act_t], partial=True)

            for q in range(4):
                if q == 0:
                    for fc in range(11):
                        ffn_in(0, fc, 0)
                        if fc % 2 == 1 and fc <= 7:
                            tt_ = 4 + fc // 2
                            if tt_ > 4:
                                ln1_post(tt_ - 1)
                            ln1_pre(tt_)
                        if fc == 9:
                            ln1_post(7)
                            for i in range(2):
                                DMA("sp", lngb[:, i:i + 1, :], ln_d[2 + i:3 + i, :].partition_broadcast(128), T_lngb,
                                    partial=(i > 0))
                    for t in range(8):
                        dump("h1_%d" % t, h1[:, t, :], h1_t[t], [128, D], F32)
                    phase_end(8)
                    for fc in range(11):
                        ffn_in(0, fc, 1)
                else:
                    for fc in range(11):
                        ffn_in(q, fc, None)
                if q < 3:
                    for cg in range(8):
                        ffn_down(q, cg, range(8))
                else:
                    for cg in range(8):
                        ffn_down(q, cg, range(4))
                    for cg in range(8):
                        ffn_down(q, cg, range(4, 8))
                        if cg % 2 == 1:
                            ln2_store(cg // 2)
            phase_end(9)
            for t in range(4, 8):
                ln2_store(t)
            fw.final_wait("sp", stores + list(dbg_out.values()))

        except _Stop:
            fw.final_wait("sp", list(dbg_out.values()))

        fw.finalize()
        block = es.enter_context(nc.Block())

        @block.tensor
        def _(e):
            fw.emit("pe", e)

        @block.scalar
        def _(e):
            fw.emit("act", e)

        @block.vector
        def _(e):
            fw.emit("dve", e)

        @block.gpsimd
        def _(e):
            fw.emit("pool", e)

        @block.sync
        def _(e):
            fw.emit("sp", e)
    return nc


_NC_CACHE = {}


def _rope_tables(pos):
    inv_freq = (1.0 / (np.float32(10000.0) ** (np.arange(0, 64, 2, dtype=np.float32) / np.float32(64)))).astype(np.float32)
    ang = pos.astype(np.float32)[:, None] * inv_freq[None, :]
    c = np.cos(ang).astype(np.float32).T
    s = np.sin(ang).astype(np.float32).T
    return np.tile(c, (4, 1)), np.tile(s, (4, 1))


def _prepare(x, w_in, mla_q_norm, mla_w_uq, mla_kv_norm, mla_w_ukv,
             diff_lambda_q1, diff_lambda_k1, diff_lambda_q2, diff_lambda_k2, diff_subln,
             w_branch_a, w_branch_b, w_out, ln1_g, ln1_b, w_ffn_in, w_ffn_down, ln2_g, ln2_b):
    f = lambda a: np.ascontiguousarray(np.asarray(a, dtype=np.float32))
    x = f(x)
    w_in0 = f(w_in)[0]
    uq = f(mla_w_uq)[0].reshape(512, 8, 192)
    w_uq_p = np.ascontiguousarray(np.concatenate([uq[:, :, :128].reshape(512, 1024), uq[:, :, 128:].reshape(512, 512)], axis=1))
    ukv = f(mla_w_ukv)[0].reshape(512, 8, 256)
    w_ukv_p = np.ascontiguousarray(np.concatenate([ukv[:, :, :128].reshape(512, 1024), ukv[:, :, 128:].reshape(512, 1024)], axis=1))
    fi = f(w_ffn_in)[0]
    w_fi_p = np.ascontiguousarray(
        np.stack([fi[:, :DFF].reshape(D, 44, 128), fi[:, DFF:].reshape(D, 44, 128)], axis=2).reshape(D, 2 * DFF))
    w_a0, w_b0, w_out0, w_fd0 = f(w_branch_a)[0], f(w_branch_b)[0], f(w_out)[0], f(w_ffn_down)[0]

    RT = np.zeros((128, 128), np.float32)
    for m in range(128):
        if m % 64 < 32:
            RT[m + 32, m] = -1.0
        else:
            RT[m - 32, m] = 1.0
    cfp = np.zeros((128, 144), np.float32)
    cfp[:, 0:128] = RT
    cfp[:, 128:132] = f(mla_q_norm)[0].reshape(4, 128).T
    cfp[:, 132:136] = f(mla_kv_norm)[0].reshape(4, 128).T
    cfp[:, 136] = f(diff_subln)[0]
    lam = np.concatenate([f(diff_lambda_q1)[0], f(diff_lambda_k1)[0], f(diff_lambda_q2)[0], f(diff_lambda_k2)[0]])[None, :]
    lam = np.ascontiguousarray(lam)
    ln = np.ascontiguousarray(np.stack([f(ln1_g)[0], f(ln1_b)[0], f(ln2_g)[0], f(ln2_b)[0]], axis=0))
    tri = np.where(np.arange(128)[:, None] <= np.arange(128)[None, :], 0.0, NEG).astype(np.float32)

    in_maps = []
    poss = []
    for c in range(8):
        b, r = c // 4, c % 4
        pos = (512 * np.arange(8)[:, None] + 128 * r + np.arange(128)[None, :]).reshape(-1)
        poss.append(pos)
        xo = x[b][pos]
        cosT, sinT = _rope_tables(pos)
        cbp = np.zeros((128, 640), np.float32)
        cbp[:, 0:128] = np.eye(128, dtype=np.float32)
        for r2 in range(4):
            blk = cbp[:, 128 + 128 * r2:256 + 128 * r2]
            if r2 > r:
                blk[:] = NEG
            elif r2 == r:
                blk[:] = tri
        in_maps.append({
            "xT": np.ascontiguousarray(xo.T), "xres": np.ascontiguousarray(xo),
            "w_in": w_in0, "w_uq": w_uq_p, "w_ukv": w_ukv_p, "w_a": w_a0, "w_b": w_b0, "w_out": w_out0,
            "w_fi": w_fi_p, "w_fd": w_fd0,
            "cs": np.ascontiguousarray(np.concatenate([cosT, sinT], axis=1)),
            "cf": cfp, "cb": cbp, "lam": lam, "ln": ln,
        })
    return in_maps, poss


def kernel(**inputs):
    in_maps, poss = _prepare(**inputs)
    if "nc" not in _NC_CACHE:
        _NC_CACHE["nc"] = build_program()
    res = run_bass_kernel_spmd(_NC_CACHE["nc"], in_maps, core_ids=list(range(8)))
    out = np.empty((2, SEQ, D), np.float32)
    for c in range(8):
        out[c // 4][poss[c]] = np.asarray(res.results[c]["out"], dtype=np.float32)
    return out
```

```python
from contextlib import ExitStack

import numpy as np
import concourse.bass as bass
import concourse.mybir as mybir
from concourse.bass_utils import run_bass_kernel_spmd

F32 = mybir.dt.float32
BF16 = mybir.dt.bfloat16
AF = mybir.ActivationFunctionType
ALU = mybir.AluOpType
AX = mybir.AxisListType

D = 2048
SEQ = 4096
NT = 1024
DFF = 5632
ALPHA = 2.0 ** 0.25
LAMBDA_INIT = 0.2
MLA_SCALE = 192.0 ** -0.5
DIFF_SCALE = 64.0 ** -0.5
RMS_EPS = 1e-6
SUBLN_EPS = 1e-5
LN_EPS = 1e-5
NEG = -30000.0
WSLOT = 4096
NWS = 6

ENGINES = ("pe", "act", "dve", "pool", "sp")


class Tile:
    def __init__(self, name, global_=False):
        self.name = name
        self.writers = []
        self.readers = []
        self.gen_deps = []
        self.sem = None
        self.count = 0
        self.global_ = global_


class Op:
    __slots__ = ("eng", "emit", "deps", "signal", "idx", "sigval", "is_dma", "tile", "semval", "inc", "nowait")

    def __init__(self, eng, emit):
        self.eng = eng
        self.emit = emit
        self.deps = []
        self.signal = False
        self.idx = -1
        self.sigval = 0
        self.is_dma = False
        self.tile = None
        self.semval = 0
        self.inc = 16
        self.nowait = False


class FW:
    def __init__(self, nc, es):
        self.nc = nc
        self.es = es
        self.ops = {e: [] for e in ENGINES}
        self.engsem = {e: es.enter_context(nc.semaphore("sem_" + e)) for e in ("pe", "act", "dve", "pool")}
        self.pending_dma = []
        self.barrier_deps = []
        self.need_barrier = {e: False for e in ENGINES}
        self.nsem = 4

    def _dedupe(self, deps, op):
        best = {}
        for d in deps:
            if d is None or d is op:
                continue
            if d.is_dma:
                key = ("t", id(d.tile))
                if key not in best or best[key].semval < d.semval:
                    best[key] = d
            else:
                if d.eng == "pe" and op.eng == "pe":
                    continue
                key = ("e", d.eng)
                if key not in best or best[key].idx < d.idx:
                    best[key] = d
        return list(best.values())

    def _add(self, op, reads, writes, partial, extra):
        deps = list(extra)
        if self.need_barrier[op.eng] and not (op.eng == "pool" and op.is_dma):
            deps += self.barrier_deps
            self.need_barrier[op.eng] = False
        for t in reads:
            deps += t.writers
        for t in writes:
            deps += t.readers
            deps += t.gen_deps
            if not partial:
                deps += t.writers
        op.deps = self._dedupe(deps, op)
        for d in op.deps:
            if not d.is_dma:
                d.signal = True
        for t in reads:
            if t not in writes:
                t.readers.append(op)
        for t in writes:
            if t.readers:
                t.gen_deps = [r for r in t.readers if r is not op]
                t.writers = [op]
                t.readers = []
            elif partial:
                t.writers.append(op)
            else:
                t.gen_deps = []
                t.writers = [op]
        op.idx = len(self.ops[op.eng])
        self.ops[op.eng].append(op)
        return op

    def op(self, eng, emit, reads=(), writes=(), partial=False, extra=()):
        return self._add(Op(eng, emit), list(reads), list(writes), partial, extra)

    def dma(self, queue, emit, dst, src=None, partial=True, extra=(), inc=16, after_barrier=False, sem_on_src=False,
            sem_tile=None):
        op = Op(queue, emit)
        op.is_dma = True
        st = sem_tile if sem_tile is not None else (src if sem_on_src else dst)
        op.tile = st
        op.inc = inc
        if st.sem is None:
            st.sem = self.es.enter_context(self.nc.semaphore("ts_" + st.name))
            self.nsem += 1
        st.count += inc
        op.semval = st.count
        ex = list(extra)
        if after_barrier:
            ex += self.barrier_deps
        self._add(op, [src] if src is not None else [], [dst], partial, ex)
        if not dst.global_:
            self.pending_dma.append(op)
        return op

    def barrier(self):
        deps = []
        for e in ("pe", "act", "dve", "pool"):
            last = None
            for o in reversed(self.ops[e]):
                if not o.is_dma and o.emit is not None:
                    last = o
                    break
            if last is not None:
                last.signal = True
                deps.append(last)
        deps += self.pending_dma
        self.pending_dma = []
        self.barrier_deps = deps
        for e in ("pe", "act", "dve", "sp", "pool"):
            self.need_barrier[e] = True

    def final_wait(self, eng, deps):
        op = Op(eng, None)
        op.deps = self._dedupe(list(deps), op)
        for d in op.deps:
            if not d.is_dma:
                d.signal = True
        op.idx = len(self.ops[eng])
        self.ops[eng].append(op)

    def finalize(self):
        for e in ENGINES:
            c = 0
            for op in self.ops[e]:
                if op.signal and not op.is_dma:
                    c += 1
                    op.sigval = c

    def emit(self, eng, handle):
        waited = {}
        for op in self.ops[eng]:
            for d in op.deps:
                if d.is_dma:
                    sem, val = d.tile.sem, d.semval
                else:
                    sem, val = self.engsem[d.eng], d.sigval
                k = id(sem)
                if waited.get(k, 0) >= val:
                    continue
                waited[k] = val
                handle.wait_ge(sem, val)
            if op.emit is None:
                continue
            ins = op.emit(handle)
            if op.is_dma:
                if op.inc == 16:
                    ins.then_inc(op.tile.sem, 16)
                else:
                    ins.then_inc(op.tile.sem)
            elif op.signal:
                ins.then_inc(self.engsem[eng], 1)


class Arena:
    def __init__(self, nc, nbytes):
        self.nbytes = nbytes
        self.t = nc.alloc_sbuf_tensor("arena", [128, nbytes // 4], F32)
        self.ap = self.t.ap()
        self.items = []

    def alloc(self, nbytes, p0, p1):
        nbytes = (nbytes + 63) // 64 * 64
        conf = sorted([(o, n) for (o, n, a, b) in self.items if not (b < p0 or a > p1)])
        off = 0
        for (o, n) in conf:
            if off + nbytes <= o:
                break
            off = max(off, o + n)
        assert off + nbytes <= self.nbytes, ("arena overflow", off, nbytes, p0, p1)
        self.items.append((off, nbytes, p0, p1))
        return off

    def f32(self, n, p0, p1):
        off = self.alloc(4 * n, p0, p1)
        return self.ap[:, off // 4: off // 4 + n]

    def bf16(self, n, p0, p1):
        off = self.alloc(2 * n, p0, p1)
        return self.ap[:, off // 4: off // 4 + n // 2].bitcast(BF16)


class _Stop(Exception):
    pass


def build_program(debug=False, stop_after=99):
    nc = bass.Bass("TRN2", target_bir_lowering=False)
    es = ExitStack()
    with es:
        def din(name, shape, dt=F32):
            return nc.dram_tensor(name, list(shape), dt, kind="ExternalInput").ap()

        xT_d = din("xT", [D, NT])
        xres_d = din("xres", [NT, D])
        w_in_d = din("w_in", [D, 8256])
        w_uq_d = din("w_uq", [512, 1536])
        w_ukv_d = din("w_ukv", [512, 2048])
        w_a_d = din("w_a", [1024, D])
        w_b_d = din("w_b", [1024, D])
        w_out_d = din("w_out", [D, D])
        w_fi_d = din("w_fi", [D, 2 * DFF])
        w_fd_d = din("w_fd", [DFF, D])
        cs_d = din("cs", [128, 2 * NT])
        cf_d = din("cf", [128, 144])
        cb_d = din("cb", [128, 640])
        lam_d = din("lam", [1, 256])
        ln_d = din("ln", [4, D])
        out_d = nc.dram_tensor("out", [NT, D], F32, kind="ExternalOutput").ap()
        GN = {"ckv": 512, "kpe": 128, "dk0": 512, "dk1": 512, "dv0": 512, "dv1": 512}
        gin = {k: nc.dram_tensor("gin_" + k, [n, NT], BF16) for k, n in GN.items()}
        gout = {k: nc.dram_tensor("gout_" + k, [4 * n, NT], BF16) for k, n in GN.items()}

        ar = Arena(nc, 207 * 1024)
        fw = FW(nc, es)

        banks = []
        bank_t = []
        for i in range(8):
            p = es.enter_context(nc.psum_tensor("ps%d" % i, [128, 512], F32))
            banks.append(p[:])
            bank_t.append(Tile("ps%d" % i))
        bank_rr = [0]

        def next_bank(choices=None):
            ch = choices if choices is not None else list(range(8))
            b = ch[bank_rr[0] % len(ch)]
            bank_rr[0] += 1
            return b

        def MM(out, lhsT, rhs, start, stop, reads, wt, partial=False):
            return fw.op("pe", lambda e: e.matmul(out, lhsT, rhs, start=start, stop=stop),
                         reads=reads, writes=[wt], partial=partial)

        def TR(out, in_, reads, wt, partial=False):
            return fw.op("pe", lambda e: e.transpose(out, in_, ident), reads=reads + [T_constb], writes=[wt],
                         partial=partial)

        def ACT(out, in_, func, reads, writes, scale=1.0, bias=0.0, partial=False):
            return fw.op("act", lambda e: e.activation(out=out, in_=in_, func=func, bias=bias, scale=scale),
                         reads=reads, writes=writes, partial=partial)

        def ACOPY(out, in_, reads, writes, partial=False):
            return fw.op("act", lambda e: e.copy(out=out, in_=in_), reads=reads, writes=writes, partial=partial)

        def VCOPY(out, in_, reads, writes, partial=False):
            return fw.op("dve", lambda e: e.tensor_copy(out=out, in_=in_), reads=reads, writes=writes, partial=partial)

        def VTT(out, in0, in1, op, reads, writes, partial=False):
            return fw.op("dve", lambda e: e.tensor_tensor(out=out, in0=in0, in1=in1, op=op),
                         reads=reads, writes=writes, partial=partial)

        def PTT(out, in0, in1, op, reads, writes, partial=False):
            return fw.op("pool", lambda e: e.tensor_tensor(out=out, in0=in0, in1=in1, op=op),
                         reads=reads, writes=writes, partial=partial)

        def VTS(out, in0, s1, s2, op0, op1, reads, writes, partial=False):
            if op1 is None:
                return fw.op("dve", lambda e: e.tensor_scalar(out=out, in0=in0, scalar1=s1, scalar2=None, op0=op0),
                             reads=reads, writes=writes, partial=partial)
            return fw.op("dve", lambda e: e.tensor_scalar(out=out, in0=in0, scalar1=s1, scalar2=s2, op0=op0, op1=op1),
                         reads=reads, writes=writes, partial=partial)

        def VSTT(out, in0, scalar, in1, op0, op1, reads, writes, partial=False):
            return fw.op("dve", lambda e: e.scalar_tensor_tensor(out=out, in0=in0, scalar=scalar, in1=in1,
                                                                 op0=op0, op1=op1),
                         reads=reads, writes=writes, partial=partial)

        def VRECIP(out, in_, reads, writes):
            return fw.op("dve", lambda e: e.reciprocal(out=out, in_=in_), reads=reads, writes=writes)

        def DMA(queue, out, in_, dst, src=None, **kw):
            return fw.dma(queue, lambda e: e.dma_start(out=out, in_=in_), dst, src=src, **kw)

        dbg_out = {}

        def dump(name, ap, tile, shape, dt):
            if not debug:
                return
            d = nc.dram_tensor("dbg_" + name, list(shape), dt, kind="ExternalOutput").ap()
            tt = Tile("dbg_" + name)
            dbg_out[name] = DMA("sp", d, ap, tt, src=tile)

        def phase_end(n):
            if stop_after == n:
                raise _Stop()

        cf = ar.f32(144, 0, 99)
        cb = ar.bf16(640, 0, 99)
        ones_f = ar.f32(128, 0, 99)
        ones_b = ar.bf16(128, 0, 99)
        sm = ar.f32(16, 0, 99)
        T_const = Tile("consts")
        T_constb = Tile("constsb")
        T_ones = Tile("ones")
        T_sm = Tile("small")
        RT = cf[:, 0:128]
        ident = cb[:, 0:128]
        wsl_ap = [ar.bf16(WSLOT, 0, 99) for _ in range(NWS)]
        wsl_t = [Tile("ws%d" % i, global_=True) for i in range(NWS)]
        ws_rr = [0]

        def ws_acquire():
            i = ws_rr[0] % NWS
            ws_rr[0] += 1
            return i

        def ws_load(slot, eoff, src3, K, C, first):
            dst = wsl_ap[slot][:, eoff:eoff + K * C].rearrange("p (k c) -> p k c", k=K)
            DMA("pool", dst, src3, wsl_t[slot], partial=not first)
            for it in list(ag_pending):
                it[0] -= 1
                if it[0] <= 0:
                    ag_pending.remove(it)
                    it[1]()
            return dst

        ag_pending = []

        def flush_ag(n=None):
            for it in list(ag_pending)[:n]:
                ag_pending.remove(it)
                it[1]()

        def wsrc(w_d, r0, nk, c0, C):
            return w_d[r0:r0 + nk * 128, c0:c0 + C].rearrange("(k p) c -> p k c", p=128)

        def load_w(src3, K, C):
            s = ws_acquire()
            return ws_load(s, 0, src3, K, C, True), s

        kvbuf = ar.bf16(2 * 16384, 5, 6)
        h1 = ar.f32(8 * D, 7, 10).rearrange("p (t n) -> p t n", t=8)
        h1T = ar.bf16(16 * NT, 7, 9).rearrange("p (k t) -> p k t", k=16)
        actT = ar.bf16(11 * NT, 8, 9).rearrange("p (c t) -> p c t", c=11)
        mergedT = ar.bf16(16 * NT, 6, 7).rearrange("p (k t) -> p k t", k=16)
        dqT = ar.bf16(8 * NT, 1, 5).rearrange("p (c t) -> p c t", c=8)
        mlaT = ar.bf16(8 * NT, 4, 6).rearrange("p (c t) -> p c t", c=8)
        diffT = ar.bf16(8 * NT, 5, 6).rearrange("p (c t) -> p c t", c=8)
        qnT = ar.bf16(8 * NT, 1, 4).rearrange("p (c t) -> p c t", c=8)
        qpeT = ar.bf16(4 * NT, 1, 4).rearrange("p (c t) -> p c t", c=4)
        sq = ar.f32(2 * 512, 1, 5).rearrange("p (c n) -> p c n", c=2)
        rstd = ar.f32(512, 1, 5)
        Pb = ar.bf16(6 * 512, 4, 5).rearrange("p (c n) -> p c n", c=6)
        rec = ar.f32(2 * 512, 4, 5).rearrange("p (c n) -> p c n", c=2)

        try:
            lamraw = ar.f32(256, 0, 3)
            cs = ar.f32(2 * NT, 0, 3)
            cosT = cs[:, 0:NT]
            sinT = cs[:, NT:2 * NT]
            xT = ar.bf16(16 * NT, 0, 3).rearrange("p (k t) -> p k t", k=16)
            xT_t = [Tile("xT%d" % i) for i in range(4)]
            T_cs = Tile("cs")
            T_lam = Tile("lamraw")

            DMA("sp", cf, cf_d[:, :], T_const)
            DMA("pool", cb, cb_d[:, :], T_constb)
            DMA("sp", lamraw.rearrange("p (o n) -> p o n", o=1), lam_d[0:1, :].partition_broadcast(128), T_lam)
            DMA("sp", cs, cs_d[:, :], T_cs)
            for i in range(4):
                DMA("pool", xT[:, 4 * i:4 * i + 4, :],
                    xT_d[512 * i:512 * (i + 1), :].rearrange("(k p) t -> p k t", p=128), xT_t[i])
            fw.op("dve", lambda e: e.memset(ones_f, 1.0), writes=[T_ones])
            fw.op("dve", lambda e: e.memset(ones_b, 1.0), writes=[T_ones], partial=True)
            lr = lamraw.rearrange("p (a n) -> p a n", a=4)
            VTT(lr[:, 0, :], lr[:, 0, :], lr[:, 1, :], ALU.mult, [T_lam], [T_lam])
            VTT(lr[:, 2, :], lr[:, 2, :], lr[:, 3, :], ALU.mult, [T_lam], [T_lam])
            fw.op("dve", lambda e: e.reduce_sum(out=sm[:, 2:3], in_=lr[:, 0, :], axis=AX.X), reads=[T_lam], writes=[T_sm])
            fw.op("dve", lambda e: e.reduce_sum(out=sm[:, 3:4], in_=lr[:, 2, :], axis=AX.X), reads=[T_lam, T_sm],
                  writes=[T_sm])
            ACT(sm[:, 4:6], sm[:, 2:4], AF.Exp, [T_sm], [T_sm])
            VTT(sm[:, 6:7], sm[:, 5:6], sm[:, 4:5], ALU.subtract, [T_sm], [T_sm])
            VTS(sm[:, 0:1], sm[:, 6:7], -LAMBDA_INIT, None, ALU.add, None, [T_sm], [T_sm])
            VTS(sm[:, 1:2], cf[:, 136:137], 1.0 - LAMBDA_INIT, None, ALU.mult, None, [T_sm, T_const], [T_sm])
            fw.op("dve", lambda e: e.memset(sm[:, 8:9], RMS_EPS), writes=[T_sm], reads=[T_sm])
            fw.op("dve", lambda e: e.memset(sm[:, 9:10], SUBLN_EPS), writes=[T_sm], reads=[T_sm])
            fw.op("dve", lambda e: e.memset(sm[:, 10:11], LN_EPS), writes=[T_sm], reads=[T_sm])
            eps_rms, eps_sub, eps_ln = sm[:, 8:9], sm[:, 9:10], sm[:, 10:11]
            neglam = sm[:, 0:1]
            gsub = sm[:, 1:2]

            phase_end(0)
            rawf = ar.f32(4 * 512, 1, 3).rearrange("p (c n) -> p c n", c=4)
            rawf_t = [Tile("rawf%d" % i) for i in range(4)]
            sq_t = [Tile("sq%d" % i) for i in range(2)]
            rstd_t = Tile("rstd")
            xf = ar.f32(2 * 512, 1, 3).rearrange("p (c n) -> p c n", c=2)
            xf_t = [Tile("xf%d" % i) for i in range(2)]
            t1 = ar.f32(2 * 512, 1, 3).rearrange("p (c n) -> p c n", c=2)
            t1_t = [Tile("t1%d" % i) for i in range(2)]
            t2 = ar.f32(2 * 512, 1, 3).rearrange("p (c n) -> p c n", c=2)
            t2_t = [Tile("t2%d" % i) for i in range(2)]
            stg = ar.bf16(4 * 512, 1, 3).rearrange("p (c n) -> p c n", c=4)
            stg_t = [Tile("stg%d" % i) for i in range(4)]
            rr = {"sq": 0, "xf": 0, "stg": 0}

            pe_defer = []

            def defer(fn):
                pe_defer.append(fn)

            def flush_deferred():
                pend = pe_defer[:]
                pe_defer.clear()
                for f in pend:
                    f()

            def flush_all():
                while pe_defer:
                    flush_deferred()

            def chain(bank, n_out, nk, lhsT_fn, rhs_fn, reads_fn, col0=0, cont=False):
                pend = pe_defer[:]
                pe_defer.clear()
                for k in range(nk):
                    MM(banks[bank][:, col0:col0 + n_out], lhsT_fn(k), rhs_fn(k), k == 0, k == nk - 1,
                       reads_fn(k), bank_t[bank], partial=(k > 0 or cont))
                for f in pend:
                    f()

            def rope_to(bank, g, dst, dst_t, after=None):
                i = rr["xf"] % 2
                rr["xf"] += 1
                ACOPY(xf[:, i, :], banks[bank][:, 0:512], [bank_t[bank]], [xf_t[i]])

                def part2():
                    b2 = next_bank()
                    MM(banks[b2][:, 0:512], RT, xf[:, i, :], True, True, [T_const, xf_t[i]], bank_t[b2])
                    VTT(t1[:, i, :], xf[:, i, :], cosT[:, g * 512:(g + 1) * 512], ALU.mult, [xf_t[i], T_cs], [t1_t[i]])
                    VTT(t2[:, i, :], banks[b2][:, 0:512], sinT[:, g * 512:(g + 1) * 512], ALU.mult,
                        [bank_t[b2], T_cs], [t2_t[i]])
                    VTT(dst, t1[:, i, :], t2[:, i, :], ALU.add, [t1_t[i], t2_t[i]], [dst_t], partial=True)
                    if after is not None:
                        after()
                defer(part2)

            def latent_norm(wa, sa_, wb, sb_, gain_col0, g, src, src_t, dst_fn, after=None):
                sbk = next_bank()

                def stat_mm(i, c):
                    MM(banks[sbk][:, 0:512], ones_f, sq[:, i, :], c == 0, c == 3, [T_ones, sq_t[i]], bank_t[sbk],
                       partial=(c > 0))

                for c in range(4):
                    w_, s_ = (wa, sa_) if c < 2 else (wb, sb_)
                    cc = c % 2
                    b = next_bank()
                    chain(b, 512, 16, lambda k: w_[:, k, cc * 128:(cc + 1) * 128],
                          lambda k: src[:, k, g * 512:(g + 1) * 512], lambda k: [src_t[k // 4], wsl_t[s_]])
                    ACOPY(rawf[:, c, :], banks[b][:, 0:512], [bank_t[b]], [rawf_t[c]])
                    i = rr["sq"] % 2
                    rr["sq"] += 1
                    ACT(sq[:, i, :], banks[b][:, 0:512], AF.Square, [bank_t[b]], [sq_t[i]])
                    defer(lambda i=i, c=c: stat_mm(i, c))

                def final():
                    ACT(rstd, banks[sbk][:, 0:512], AF.Ln, [bank_t[sbk], T_sm], [rstd_t], scale=1.0 / 512.0,
                        bias=eps_rms)
                    ACT(rstd, rstd, AF.Exp, [rstd_t], [rstd_t], scale=-0.5)
                    for c in range(4):
                        dst, dt_ = dst_fn(c)
                        VSTT(dst, rawf[:, c, :], cf[:, gain_col0 + c:gain_col0 + c + 1], rstd, ALU.mult, ALU.mult,
                             [rawf_t[c], rstd_t, T_const], [dt_], partial=True)
                    if after is not None:
                        after()
                defer(final)

            def new_stg():
                i = rr["stg"] % 4
                rr["stg"] += 1
                return i

            T_gin = {k: Tile("gin_" + k) for k in GN}
            T_gout = {k: Tile("gout_" + k) for k in GN}

            def all_gather(k):
                def issue():
                    fw.dma("pool", lambda e: e.collective_compute(
                        "AllGather", ALU.bypass, replica_groups=RG, ins=[gin[k].ap().opt()],
                        outs=[gout[k].ap().opt()]), T_gout[k], src=T_gin[k], partial=False, inc=1)
                ag_pending.append([10 ** 9, issue])
            RG = [[0, 1, 2, 3], [4, 5, 6, 7]]

            w_ckv, s_ckv = load_w(wsrc(w_in_d, 0, 16, 512, 256), 16, 256)
            w_ckv2, s_ckv2 = load_w(wsrc(w_in_d, 0, 16, 768, 256), 16, 256)
            def ckv_group(g):
                used = []

                def dst_fn(c):
                    i = new_stg()
                    used.append((c, i))
                    return stg[:, i, :], stg_t[i]

                def after():
                    for (c, i) in used:
                        DMA("sp", gin["ckv"].ap()[c * 128:(c + 1) * 128, g * 512:(g + 1) * 512], stg[:, i, :],
                            T_gin["ckv"], src=stg_t[i], sem_on_src=True)
                latent_norm(w_ckv, s_ckv, w_ckv2, s_ckv2, 132, g, xT, xT_t, dst_fn, after)
            for g in range(2):
                ckv_group(g)
            phase_end(0.1)
            s_kr = ws_acquire()
            w_kr = wsl_ap[s_kr][:, 0:16 * 128].rearrange("p (k c) -> p k c", k=16)
            for h2 in range(2):
                DMA("pool", w_kr[:, :, h2 * 64:(h2 + 1) * 64], wsrc(w_in_d, 0, 16, 1024, 64), wsl_t[s_kr],
                    partial=(h2 > 0))
            def roped_store(b, g, key, r0):
                ii = new_stg()

                def dst_after():
                    DMA("sp", gin[key].ap()[r0:r0 + 128, g * 512:(g + 1) * 512], stg[:, ii, :], T_gin[key],
                        src=stg_t[ii], sem_on_src=True)
                rope_to(b, g, stg[:, ii, :], stg_t[ii], after=dst_after)

            for g in range(2):
                b = next_bank()
                chain(b, 512, 16, lambda k: w_kr[:, k, :], lambda k: xT[:, k, g * 512:(g + 1) * 512],
                      lambda k: [xT_t[k // 4], wsl_t[s_kr]])
                roped_store(b, g, "kpe", 0)
            flush_all()
            all_gather("ckv")
            phase_end(0.2)
            all_gather("kpe")
            phase_end(0.3)
            for cg in range(4):
                w_dk, s_dk = load_w(wsrc(w_in_d, 0, 16, 2112 + cg * 256, 256), 16, 256)
                if cg == 3:
                    flush_ag(2)
                for cc in range(2):
                    c = cg * 2 + cc
                    for g in range(2):
                        b = next_bank()
                        chain(b, 512, 16, lambda k: w_dk[:, k, cc * 128:(cc + 1) * 128],
                              lambda k: xT[:, k, g * 512:(g + 1) * 512], lambda k: [xT_t[k // 4], wsl_t[s_dk]])
                        roped_store(b, g, "dk%d" % (c // 4), (c % 4) * 128)
                if cg % 2 == 1:
                    flush_all()
                    all_gather("dk%d" % (cg // 2))
            phase_end(0.4)
            for cg in range(4):
                w_dv, s_dv = load_w(wsrc(w_in_d, 0, 16, 3136 + cg * 256, 256), 16, 256)
                for tp in range(4):
                    b = next_bank()
                    for hh in range(2):
                        t = tp * 2 + hh
                        chain(b, 256, 16, lambda k: xT[:, k, t * 128:(t + 1) * 128], lambda k: w_dv[:, k, :],
                              lambda k: [xT_t[k // 4], wsl_t[s_dv]], col0=hh * 256, cont=(hh > 0))
                    i = new_stg()
                    ACOPY(stg[:, i, :], banks[b][:, 0:512], [bank_t[b]], [stg_t[i]])
                    for hh in range(2):
                        t = tp * 2 + hh
                        kk_ = "dv%d" % (t // 4)
                        DMA("sp", gin[kk_].ap()[(t % 4) * 128:(t % 4 + 1) * 128, cg * 256:(cg + 1) * 256],
                            stg[:, i, hh * 256:(hh + 1) * 256], T_gin[kk_], src=stg_t[i], sem_on_src=True)
            phase_end(0.5)
            all_gather("dv0")
            all_gather("dv1")

            phase_end(1)
            cqnT = ar.bf16(4 * NT, 1, 3).rearrange("p (c t) -> p c t", c=4)
            cqn_t = [Tile("cqnT")]
            qn_t = [Tile("qnT%d" % h) for h in range(8)]
            qpe_t = [Tile("qpeT%d" % j) for j in range(4)]
            dq_t = [Tile("dqT%d" % h) for h in range(8)]

            w_cq, s_cq = load_w(wsrc(w_in_d, 0, 16, 0, 256), 16, 256)
            w_cq2, s_cq2 = load_w(wsrc(w_in_d, 0, 16, 256, 256), 16, 256)
            flush_ag(2)
            for g in range(2):
                latent_norm(w_cq, s_cq, w_cq2, s_cq2, 128, g, xT, xT_t,
                            lambda c, g=g: (cqnT[:, c, g * 512:(g + 1) * 512], cqn_t[0]))
            flush_all()
            w_uqn, s_uqn = load_w(wsrc(w_uq_d, 0, 4, 0, 1024), 4, 1024)
            w_uqr, s_uqr = load_w(wsrc(w_uq_d, 0, 4, 1024, 512), 4, 512)
            for h in range(8):
                for g in range(2):
                    b = next_bank()
                    chain(b, 512, 4, lambda k: w_uqn[:, k, h * 128:(h + 1) * 128],
                          lambda k: cqnT[:, k, g * 512:(g + 1) * 512], lambda k: [cqn_t[0], wsl_t[s_uqn]])
                    ACOPY(qnT[:, h, g * 512:(g + 1) * 512], banks[b][:, 0:512], [bank_t[b]], [qn_t[h]], partial=True)
            for j in range(4):
                for g in range(2):
                    b = next_bank()
                    chain(b, 512, 4, lambda k: w_uqr[:, k, j * 128:(j + 1) * 128],
                          lambda k: cqnT[:, k, g * 512:(g + 1) * 512], lambda k: [cqn_t[0], wsl_t[s_uqr]])
                    rope_to(b, g, qpeT[:, j, g * 512:(g + 1) * 512], qpe_t[j])

            phase_end(2)
            for cg in range(4):
                w_dq, s_dq = load_w(wsrc(w_in_d, 0, 16, 1088 + cg * 256, 256), 16, 256)
                if cg == 1:
                    flush_ag()
                for cc in range(2):
                    h = cg * 2 + cc
                    for g in range(2):
                        b = next_bank()
                        chain(b, 512, 16, lambda k: w_dq[:, k, cc * 128:(cc + 1) * 128],
                              lambda k: xT[:, k, g * 512:(g + 1) * 512], lambda k: [xT_t[k // 4], wsl_t[s_dq]])
                        rope_to(b, g, dqT[:, h, g * 512:(g + 1) * 512], dq_t[h])

            flush_all()
            flush_ag()
            phase_end(3)
            fw.barrier()

            ckv_all = ar.bf16(4 * SEQ, 4, 4).rearrange("p (c t) -> p c t", c=4)
            T_ckv_all = Tile("ckv_all")
            kpe_all = ar.bf16(SEQ, 4, 4)
            knT = ar.bf16(2 * SEQ, 4, 4).rearrange("p (c t) -> p c t", c=2)
            kn_t = [Tile("knT%d" % i) for i in range(2)]
            Vp = ar.bf16(32 * 256, 4, 4).rearrange("p (t c) -> p t c", t=32)
            Vp_t = Tile("Vp")
            P_t = [Tile("P%d" % i) for i in range(6)]
            rec_t = [Tile("rec%d" % i) for i in range(2)]
            mla_t = [Tile("mlaT%d" % h) for h in range(8)]
            diff_t = [Tile("diffT%d" % h) for h in range(8)]

            for r_ in range(4):
                DMA("sp", ckv_all[:, :, r_ * NT:(r_ + 1) * NT],
                    gout["ckv"].ap()[r_ * 512:(r_ + 1) * 512, :].rearrange("(c p) t -> p c t", p=128),
                    T_ckv_all, src=T_gout["ckv"])
                DMA("sp", kpe_all[:, r_ * NT:(r_ + 1) * NT], gout["kpe"].ap()[r_ * 128:(r_ + 1) * 128, :],
                    T_ckv_all, src=T_gout["kpe"])
            w_uk, s_uk = load_w(wsrc(w_ukv_d, 0, 4, 0, 1024), 4, 1024)
            w_uv, s_uv = load_w(wsrc(w_ukv_d, 0, 4, 1024, 1024), 4, 1024)

            def steps_for(G):
                st = []
                for r_ in range(4):
                    for j_ in range(4 * G + 4):
                        a = j_ - 4 * G
                        st.append((r_, j_, 128 * a if a >= 0 else 0, a >= 0))
                return st

            S_BANKS_MLA = [0, 1, 2]
            O_BANK, SUM_BANK = 3, 4
            PROD_BANKS = [5, 6, 7]
            p_rr = [0]
            rec_rr = [0]

            def mla_attention(h, hh, ks, G):
                j, eh = h // 2, h % 2
                steps = steps_for(G)
                ns = len(steps)

                def emit_S(s):
                    r_, j_, col0, dg = steps[s]
                    n = 512 - col0
                    tok = r_ * NT + j_ * 128
                    sbk = S_BANKS_MLA[s % 3]
                    q0 = G * 512 + col0
                    MM(banks[sbk][:, 0:n], knT[:, ks, tok:tok + 128], qnT[:, h, q0:q0 + n], True, False,
                       [kn_t[ks], qn_t[h]], bank_t[sbk])
                    MM(banks[sbk][:, 0:n], kpe_all[eh * 64:(eh + 1) * 64, tok:tok + 128],
                       qpeT[eh * 64:(eh + 1) * 64, j, q0:q0 + n], False, not dg,
                       [T_ckv_all, qpe_t[j]], bank_t[sbk], partial=True)
                    if dg:
                        MM(banks[sbk][:, 0:128], ident, cb[:, 128 + 128 * r_:256 + 128 * r_], False, True,
                           [T_constb], bank_t[sbk], partial=True)

                emit_S(0)
                if ns > 1:
                    emit_S(1)
                for s in range(ns):
                    r_, j_, col0, dg = steps[s]
                    n = 512 - col0
                    sbk = S_BANKS_MLA[s % 3]
                    pi = p_rr[0] % 6
                    p_rr[0] += 1
                    ACT(Pb[:, pi, 0:n], banks[sbk][:, 0:n], AF.Exp, [bank_t[sbk]], [P_t[pi]], scale=MLA_SCALE)
                    if s + 2 < ns:
                        emit_S(s + 2)
                    tt = r_ * 8 + j_
                    MM(banks[O_BANK][:, col0:512], Vp[:, tt, hh * 128:(hh + 1) * 128], Pb[:, pi, 0:n], s == 0, s == ns - 1,
                       [Vp_t, P_t[pi]], bank_t[O_BANK], partial=(s > 0))
                    MM(banks[SUM_BANK][:, col0:512], ones_b, Pb[:, pi, 0:n], s == 0, s == ns - 1,
                       [T_ones, P_t[pi]], bank_t[SUM_BANK], partial=(s > 0))
                ri = rec_rr[0] % 2
                rec_rr[0] += 1
                ACT(rec[:, ri, :], banks[SUM_BANK][:, 0:512], AF.Ln, [bank_t[SUM_BANK]], [rec_t[ri]])
                ACT(rec[:, ri, :], rec[:, ri, :], AF.Exp, [rec_t[ri]], [rec_t[ri]], scale=-1.0)
                VTT(mlaT[:, h, G * 512:(G + 1) * 512], banks[O_BANK][:, 0:512], rec[:, ri, :], ALU.mult,
                    [bank_t[O_BANK], rec_t[ri]], [mla_t[h]], partial=True)

            for hp in range(4):
                for tp in range(16):
                    b = next_bank(PROD_BANKS)
                    for hh in range(2):
                        tt = tp * 2 + hh
                        chain(b, 256, 4, lambda k: ckv_all[:, k, tt * 128:(tt + 1) * 128],
                              lambda k: w_uv[:, k, hp * 256:(hp + 1) * 256], lambda k: [T_ckv_all, wsl_t[s_uv]],
                              col0=hh * 256, cont=(hh > 0))
                    VCOPY(Vp[:, 2 * tp:2 * tp + 2, :], banks[b][:, 0:512].rearrange("p (t c) -> p t c", t=2),
                          [bank_t[b]], [Vp_t], partial=True)
                for hh in range(2):
                    h = hp * 2 + hh
                    ks = h % 2
                    for tg in range(8):
                        b = next_bank(PROD_BANKS)
                        chain(b, 512, 4, lambda k: w_uk[:, k, h * 128:(h + 1) * 128],
                              lambda k: ckv_all[:, k, tg * 512:(tg + 1) * 512], lambda k: [T_ckv_all, wsl_t[s_uk]])
                        VCOPY(knT[:, ks, tg * 512:(tg + 1) * 512], banks[b][:, 0:512], [bank_t[b]], [kn_t[ks]],
                              partial=True)
                    for G in range(2):
                        mla_attention(h, hh, ks, G)
            for h in range(8):
                dump("mlaT%d" % h, mlaT[:, h, :], mla_t[h], [128, NT], BF16)

            phase_end(4)
            fw.barrier()

            kvs = [kvbuf[:, sl_ * 16384:(sl_ + 1) * 16384] for sl_ in range(2)]
            dkT_s = [kvs[sl_][:, 0:8192].rearrange("p (c t) -> p c t", c=2) for sl_ in range(2)]
            dvp_s = [kvs[sl_][:, 8192:16384].rearrange("p (t c) -> p t c", t=32) for sl_ in range(2)]
            kv_t = [Tile("kv%d" % i) for i in range(2)]
            xT2 = kvs[0].rearrange("p (k t) -> p k t", k=16)
            T_xT2sem = Tile("xT2sem")
            dsc = ar.f32(3 * 512, 5, 5).rearrange("p (c n) -> p c n", c=3)
            dsc_t = [Tile("dsc%d" % i) for i in range(3)]
            S_BANKS_D = [0, 1, 2]
            STAT_BANK_D = 3
            O1B, O2B, S1B, S2B = 4, 5, 6, 7

            diff_stages = [{}]

            def diff_attention(h, hh, sl, G):
                steps = steps_for(G)
                ns = len(steps)
                cur_stages = diff_stages[0]
                diff_stages[0] = {}

                def emit_S(s):
                    r_, j_, col0, dg = steps[s]
                    n = 512 - col0
                    tok = r_ * NT + j_ * 128
                    q0 = G * 512 + col0
                    for m in range(2):
                        sbk = S_BANKS_D[(2 * s + m) % 3]
                        MM(banks[sbk][:, 0:n], dkT_s[sl][m * 64:(m + 1) * 64, hh, tok:tok + 128],
                           dqT[m * 64:(m + 1) * 64, h, q0:q0 + n], True, not dg, [kv_t[sl], dq_t[h]], bank_t[sbk])
                        if dg:
                            MM(banks[sbk][:, 0:128], ident, cb[:, 128 + 128 * r_:256 + 128 * r_], False, True,
                               [T_constb], bank_t[sbk], partial=True)

                emit_S(0)
                for s in range(ns):
                    r_, j_, col0, dg = steps[s]
                    n = 512 - col0
                    pis = []
                    for m in range(2):
                        sbk = S_BANKS_D[(2 * s + m) % 3]
                        pi = p_rr[0] % 6
                        p_rr[0] += 1
                        pis.append(pi)
                        ACT(Pb[:, pi, 0:n], banks[sbk][:, 0:n], AF.Exp, [bank_t[sbk]], [P_t[pi]], scale=DIFF_SCALE)
                    if s + 1 < ns:
                        emit_S(s + 1)
                    tt = r_ * 8 + j_
                    for m in range(2):
                        pi = pis[m]
                        ob, sb_ = (O1B, S1B) if m == 0 else (O2B, S2B)
                        MM(banks[ob][:, col0:512], dvp_s[sl][:, tt, hh * 128:(hh + 1) * 128], Pb[:, pi, 0:n],
                           s == 0, s == ns - 1, [kv_t[sl], P_t[pi]], bank_t[ob], partial=(s > 0))
                        MM(banks[sb_][:, col0:512], ones_b, Pb[:, pi, 0:n], s == 0, s == ns - 1,
                           [T_ones, P_t[pi]], bank_t[sb_], partial=(s > 0))
                    for f in cur_stages.pop(s, []):
                        f()
                for s_ in sorted(cur_stages):
                    for f in cur_stages[s_]:
                        f()
                VCOPY(dsc[:, 0, :], banks[O1B][:, 0:512], [bank_t[O1B]], [dsc_t[0]])
                VCOPY(dsc[:, 1, :], banks[O2B][:, 0:512], [bank_t[O2B]], [dsc_t[1]])
                VCOPY(rec[:, 0, :], banks[S1B][:, 0:512], [bank_t[S1B]], [rec_t[0]])
                VCOPY(rec[:, 1, :], banks[S2B][:, 0:512], [bank_t[S2B]], [rec_t[1]])
                stb = STAT_BANK_D

                def st_a():
                    ACT(rec[:, 0, :], rec[:, 0, :], AF.Ln, [rec_t[0]], [rec_t[0]])
                    ACT(rec[:, 1, :], rec[:, 1, :], AF.Ln, [rec_t[1]], [rec_t[1]])

                def st_b():
                    ACT(rec[:, 0, :], rec[:, 0, :], AF.Exp, [rec_t[0]], [rec_t[0]], scale=-1.0)
                    ACT(rec[:, 1, :], rec[:, 1, :], AF.Exp, [rec_t[1]], [rec_t[1]], scale=-1.0)
                    VTT(dsc[:, 0, :], dsc[:, 0, :], rec[:, 0, :], ALU.mult, [dsc_t[0], rec_t[0]], [dsc_t[0]])
                    VTT(dsc[:, 1, :], dsc[:, 1, :], rec[:, 1, :], ALU.mult, [dsc_t[1], rec_t[1]], [dsc_t[1]])
                    VSTT(dsc[:, 2, :], dsc[:, 1, :], neglam, dsc[:, 0, :], ALU.mult, ALU.add,
                         [dsc_t[0], dsc_t[1], T_sm], [dsc_t[2]])
                    VTT(sq[:, 0, :], dsc[:, 2, :], dsc[:, 2, :], ALU.mult, [dsc_t[2]], [sq_t[0]])

                def st_c():
                    MM(banks[stb][:, 0:512], ones_f, sq[:, 0, :], True, True, [T_ones, sq_t[0]], bank_t[stb])
                    ACT(rstd, banks[stb][:, 0:512], AF.Ln, [bank_t[stb], T_sm], [rstd_t], scale=1.0 / 128.0,
                        bias=eps_sub)

                def st_d():
                    ACT(rstd, rstd, AF.Exp, [rstd_t], [rstd_t], scale=-0.5)
                    VSTT(diffT[:, h, G * 512:(G + 1) * 512], dsc[:, 2, :], gsub, rstd, ALU.mult, ALU.mult,
                         [dsc_t[2], rstd_t, T_sm], [diff_t[h]], partial=True)
                diff_stages[0] = {1: [st_a], 2: [st_b], 5: [st_c], 7: [st_d]}

            for hp in range(4):
                sl = hp % 2
                for r_ in range(4):
                    kk_ = "dk%d" % (hp // 2)
                    r0_ = r_ * 512 + (hp % 2) * 256
                    DMA("sp", dkT_s[sl][:, :, r_ * NT:(r_ + 1) * NT],
                        gout[kk_].ap()[r0_:r0_ + 256, :].rearrange("(c p) t -> p c t", p=128),
                        kv_t[sl], src=T_gout[kk_])
                    for tf in range(2):
                        kv_ = "dv%d" % tf
                        DMA("sp", dvp_s[sl][:, r_ * 8 + 4 * tf:r_ * 8 + 4 * tf + 4, :],
                            gout[kv_].ap()[r_ * 512:(r_ + 1) * 512, hp * 256:(hp + 1) * 256].rearrange(
                                "(j p) c -> p j c", p=128), kv_t[sl], src=T_gout[kv_])
                if hp == 3:
                    for i in range(4):
                        DMA("pool", xT2[:, 4 * i:4 * i + 4, :],
                            xT_d[512 * i:512 * (i + 1), :].rearrange("(k p) t -> p k t", p=128), kv_t[0],
                            sem_tile=T_xT2sem)
                for hh in range(2):
                    for G in range(2):
                        diff_attention(hp * 2 + hh, hh, sl, G)
            for s_ in sorted(diff_stages[0]):
                for f in diff_stages[0][s_]:
                    f()
            flush_all()
            for h in range(8):
                dump("diffT%d" % h, diffT[:, h, :], diff_t[h], [128, NT], BF16)

            phase_end(5)
            fw.barrier()

            mg_t = [Tile("mg%d" % i) for i in range(16)]
            esc = ar.f32(8 * 512, 6, 6).rearrange("p (c n) -> p c n", c=8)
            esc_t = [Tile("esc%d" % i) for i in range(8)]
            e_rr = [0]
            for fg in range(8):
                s_ab = ws_acquire()
                w_a = ws_load(s_ab, 0, wsrc(w_a_d, 0, 8, fg * 256, 256), 8, 256, True)
                w_b = ws_load(s_ab, 2048, wsrc(w_b_d, 0, 8, fg * 256, 256), 8, 256, False)
                w_ga, s_ga = load_w(wsrc(w_in_d, 0, 16, 4160 + fg * 256, 256), 16, 256)
                w_gb, s_gb = load_w(wsrc(w_in_d, 0, 16, 6208 + fg * 256, 256), 16, 256)
                for cc in range(2):
                    c = fg * 2 + cc
                    for g in range(2):
                        bya, byb, bga, bgb = next_bank(), next_bank(), next_bank(), next_bank()
                        chain(bya, 512, 8, lambda k: w_a[:, k, cc * 128:(cc + 1) * 128],
                              lambda k: mlaT[:, k, g * 512:(g + 1) * 512], lambda k: [mla_t[k], wsl_t[s_ab]])
                        chain(byb, 512, 8, lambda k: w_b[:, k, cc * 128:(cc + 1) * 128],
                              lambda k: diffT[:, k, g * 512:(g + 1) * 512], lambda k: [diff_t[k], wsl_t[s_ab]])
                        chain(bga, 512, 16, lambda k: w_ga[:, k, cc * 128:(cc + 1) * 128],
                              lambda k: xT2[:, k, g * 512:(g + 1) * 512], lambda k: [kv_t[0], wsl_t[s_ga]])
                        chain(bgb, 512, 16, lambda k: w_gb[:, k, cc * 128:(cc + 1) * 128],
                              lambda k: xT2[:, k, g * 512:(g + 1) * 512], lambda k: [kv_t[0], wsl_t[s_gb]])
                        o = 4 * (e_rr[0] % 2)
                        e_rr[0] += 1
                        ACT(esc[:, o, :], banks[bga][:, 0:512], AF.Sigmoid, [bank_t[bga]], [esc_t[o]])
                        ACT(esc[:, o + 1, :], banks[bgb][:, 0:512], AF.Sigmoid, [bank_t[bgb]], [esc_t[o + 1]])
                        VTT(esc[:, o + 2, :], banks[bya][:, 0:512], esc[:, o, :], ALU.mult,
                            [bank_t[bya], esc_t[o]], [esc_t[o + 2]])
                        VTT(esc[:, o + 3, :], banks[byb][:, 0:512], esc[:, o + 1, :], ALU.mult,
                            [bank_t[byb], esc_t[o + 1]], [esc_t[o + 3]])
                        VTT(mergedT[:, c, g * 512:(g + 1) * 512], esc[:, o + 2, :], esc[:, o + 3, :], ALU.add,
                            [esc_t[o + 2], esc_t[o + 3]], [mg_t[c]], partial=True)
            for c in range(16):
                dump("mg%d" % c, mergedT[:, c, :], mg_t[c], [128, NT], BF16)

            phase_end(6)
            fw.barrier()

            h1_t = [Tile("h1_%d" % t) for t in range(8)]
            lngb = ar.f32(2 * D, 7, 10).rearrange("p (c n) -> p c n", c=2)
            sg = ar.f32(2 * 512, 7, 9).rearrange("p (c n) -> p c n", c=2)
            T_lngb = Tile("lngb")
            lnst = ar.f32(64, 7, 10)
            lnst_t = Tile("lnst")
            hb = ar.bf16(D, 7, 9)
            hb_t = Tile("hb")
            h1T_t = [Tile("h1T_%d" % g) for g in range(2)]
            for t in range(8):
                DMA("sp", h1[:, t, :], xres_d[t * 128:(t + 1) * 128, :], h1_t[t], after_barrier=True)
            for i in range(2):
                DMA("sp", lngb[:, i:i + 1, :], ln_d[i:i + 1, :].partition_broadcast(128), T_lngb, after_barrier=True)

            def layer_norm(t, gb, gb_t):
                for c in range(4):
                    cc = c
                    fw.op("dve", lambda e, cc=cc: e.bn_stats(out=lnst[:, 6 * cc:6 * cc + 6],
                                                             in_=h1[:, t, cc * 512:(cc + 1) * 512]),
                          reads=[h1_t[t], lnst_t] if c == 0 else [h1_t[t]], writes=[lnst_t], partial=(c > 0))
                fw.op("dve", lambda e: e.bn_aggr(out=lnst[:, 24:26], in_=lnst[:, 0:24]), reads=[lnst_t], writes=[lnst_t])
                ACT(lnst[:, 26:27], lnst[:, 25:26], AF.Ln, [lnst_t, T_sm], [lnst_t], bias=eps_ln)
                ACT(lnst[:, 26:27], lnst[:, 26:27], AF.Exp, [lnst_t], [lnst_t], scale=-0.5)
                VSTT(lnst[:, 27:28], lnst[:, 24:25], -1.0, lnst[:, 26:27], ALU.mult, ALU.mult, [lnst_t], [lnst_t])
                ACT(h1[:, t, :], h1[:, t, :], AF.Identity, [h1_t[t], lnst_t], [h1_t[t]],
                    scale=lnst[:, 26:27], bias=lnst[:, 27:28])
                VTT(h1[:, t, :], h1[:, t, :], gb[:, 0, :], ALU.mult, [h1_t[t], gb_t], [h1_t[t]])
                VTT(h1[:, t, :], h1[:, t, :], gb[:, 1, :], ALU.add, [h1_t[t], gb_t], [h1_t[t]])

            def ln1_pre(t):
                layer_norm(t, lngb, T_lngb)
                ACOPY(hb, h1[:, t, :], [h1_t[t]], [hb_t])

            def ln1_post(t):
                for kq in range(4):
                    b = next_bank()
                    pb = banks[b].bitcast(BF16)
                    for kk in range(4):
                        k = kq * 4 + kk
                        TR(pb[:, kk * 128:(kk + 1) * 128], hb[:, k * 128:(k + 1) * 128], [hb_t], bank_t[b],
                           partial=(kk > 0))
                    VCOPY(h1T[:, 4 * kq:4 * kq + 4, t * 128:(t + 1) * 128],
                          pb[:, 0:512].rearrange("p (k n) -> p k n", k=4), [bank_t[b]], [h1T_t[t // 4]], partial=True)

            for half in range(2):
                for cg in range(8):
                    w_o, s_o = load_w(wsrc(w_out_d, 0, 16, cg * 256, 256), 16, 256)
                    for t in range(4 * half, 4 * half + 4):
                        b = next_bank()
                        chain(b, 256, 16, lambda k: mergedT[:, k, t * 128:(t + 1) * 128], lambda k: w_o[:, k, :],
                              lambda k: [mg_t[k], wsl_t[s_o]])
                        VSTT(h1[:, t, cg * 256:(cg + 1) * 256], h1[:, t, cg * 256:(cg + 1) * 256], ALPHA,
                             banks[b][:, 0:256], ALU.mult, ALU.add, [bank_t[b], h1_t[t]], [h1_t[t]])
                    if half == 1 and cg % 2 == 1:
                        tt_ = cg // 2
                        if tt_ > 0:
                            ln1_post(tt_ - 1)
                        ln1_pre(tt_)
            ln1_post(3)
            phase_end(7)

            act_t = Tile("actT")
            sg_t = [Tile("sg%d" % i) for i in range(2)]
            sg_rr = [0]
            out_t = [Tile("out%d" % t) for t in range(8)]
            stores = []

            def ln2_store(t):
                layer_norm(t, lngb, T_lngb)
                stores.append(DMA("sp", out_d[t * 128:(t + 1) * 128, :], h1[:, t, :], out_t[t], src=h1_t[t]))

            def ffn_down(q, cg, ts):
                w_d, s_d = load_w(wsrc(w_fd_d, q * 1408, 11, cg * 256, 256), 11, 256)
                for t in ts:
                    b = next_bank()
                    chain(b, 256, 11, lambda k: actT[:, k, t * 128:(t + 1) * 128], lambda k: w_d[:, k, :],
                          lambda k: [act_t, wsl_t[s_d]])
                    hsl = h1[:, t, cg * 256:(cg + 1) * 256]
                    if q == 0:
                        VSTT(hsl, hsl, ALPHA, banks[b][:, 0:256], ALU.mult, ALU.add, [bank_t[b], h1_t[t]], [h1_t[t]])
                    else:
                        VTT(hsl, hsl, banks[b][:, 0:256], ALU.add, [bank_t[b], h1_t[t]], [h1_t[t]])

            def ffn_in(q, fc, g):
                fcg = q * 11 + fc
                w_f, s_f = load_w(wsrc(w_fi_d, 0, 16, fcg * 256, 256), 16, 256)
                gs = [g] if g is not None else [0, 1]
                for g_ in gs:
                    bg, bu = next_bank(), next_bank()
                    chain(bg, 512, 16, lambda k: w_f[:, k, 0:128], lambda k: h1T[:, k, g_ * 512:(g_ + 1) * 512],
                          lambda k: [h1T_t[g_], wsl_t[s_f]])
                    chain(bu, 512, 16, lambda k: w_f[:, k, 128:256], lambda k: h1T[:, k, g_ * 512:(g_ + 1) * 512],
                          lambda k: [h1T_t[g_], wsl_t[s_f]])
                    si = sg_rr[0] % 2
                    sg_rr[0] += 1
                    ACT(sg[:, si, :], banks[bg][:, 0:512], AF.Silu, [bank_t[bg]], [sg_t[si]])
                    VTT(actT[:, fc, g_ * 512:(g_ + 1) * 512], banks[bu][:, 0:512], sg[:, si, :], ALU.mult,
                        [bank_t[bu], sg_t[si]], [act_t], partial=True)

            for q in range(4):
                if q == 0:
                    for fc in range(11):
                        ffn_in(0, fc, 0)
                        if fc % 2 == 1 and fc <= 7:
                            tt_ = 4 + fc // 2
                            if tt_ > 4:
                                ln1_post(tt_ - 1)
                            ln1_pre(tt_)
                        if fc == 9:
                            ln1_post(7)
                            for i in range(2):
                                DMA("sp", lngb[:, i:i + 1, :], ln_d[2 + i:3 + i, :].partition_broadcast(128), T_lngb,
                                    partial=(i > 0))
                    for t in range(8):
                        dump("h1_%d" % t, h1[:, t, :], h1_t[t], [128, D], F32)
                    phase_end(8)
                    for fc in range(11):
                        ffn_in(0, fc, 1)
                else:
                    for fc in range(11):
                        ffn_in(q, fc, None)
                if q < 3:
                    for cg in range(8):
                        ffn_down(q, cg, range(8))
                else:
                    for cg in range(8):
                        ffn_down(q, cg, range(4))
                    for cg in range(8):
                        ffn_down(q, cg, range(4, 8))
                        if cg % 2 == 1:
                            ln2_store(cg // 2)
            phase_end(9)
            for t in range(4, 8):
                ln2_store(t)
            fw.final_wait("sp", stores + list(dbg_out.values()))

        except _Stop:
            fw.final_wait("sp", list(dbg_out.values()))

        fw.finalize()
        block = es.enter_context(nc.Block())

        @block.tensor
        def _(e):
            fw.emit("pe", e)

        @block.scalar
        def _(e):
            fw.emit("act", e)

        @block.vector
        def _(e):
            fw.emit("dve", e)

        @block.gpsimd
        def _(e):
            fw.emit("pool", e)

        @block.sync
        def _(e):
            fw.emit("sp", e)
    return nc


_NC_CACHE = {}


def _rope_tables(pos):
    inv_freq = (1.0 / (np.float32(10000.0) ** (np.arange(0, 64, 2, dtype=np.float32) / np.float32(64)))).astype(np.float32)
    ang = pos.astype(np.float32)[:, None] * inv_freq[None, :]
    c = np.cos(ang).astype(np.float32).T
    s = np.sin(ang).astype(np.float32).T
    return np.tile(c, (4, 1)), np.tile(s, (4, 1))


def _prepare(x, w_in, mla_q_norm, mla_w_uq, mla_kv_norm, mla_w_ukv,
             diff_lambda_q1, diff_lambda_k1, diff_lambda_q2, diff_lambda_k2, diff_subln,
             w_branch_a, w_branch_b, w_out, ln1_g, ln1_b, w_ffn_in, w_ffn_down, ln2_g, ln2_b):
    f = lambda a: np.ascontiguousarray(np.asarray(a, dtype=np.float32))
    x = f(x)
    w_in0 = f(w_in)[0]
    uq = f(mla_w_uq)[0].reshape(512, 8, 192)
    w_uq_p = np.ascontiguousarray(np.concatenate([uq[:, :, :128].reshape(512, 1024), uq[:, :, 128:].reshape(512, 512)], axis=1))
    ukv = f(mla_w_ukv)[0].reshape(512, 8, 256)
    w_ukv_p = np.ascontiguousarray(np.concatenate([ukv[:, :, :128].reshape(512, 1024), ukv[:, :, 128:].reshape(512, 1024)], axis=1))
    fi = f(w_ffn_in)[0]
    w_fi_p = np.ascontiguousarray(
        np.stack([fi[:, :DFF].reshape(D, 44, 128), fi[:, DFF:].reshape(D, 44, 128)], axis=2).reshape(D, 2 * DFF))
    w_a0, w_b0, w_out0, w_fd0 = f(w_branch_a)[0], f(w_branch_b)[0], f(w_out)[0], f(w_ffn_down)[0]

    RT = np.zeros((128, 128), np.float32)
    for m in range(128):
        if m % 64 < 32:
            RT[m + 32, m] = -1.0
        else:
            RT[m - 32, m] = 1.0
    cfp = np.zeros((128, 144), np.float32)
    cfp[:, 0:128] = RT
    cfp[:, 128:132] = f(mla_q_norm)[0].reshape(4, 128).T
    cfp[:, 132:136] = f(mla_kv_norm)[0].reshape(4, 128).T
    cfp[:, 136] = f(diff_subln)[0]
    lam = np.concatenate([f(diff_lambda_q1)[0], f(diff_lambda_k1)[0], f(diff_lambda_q2)[0], f(diff_lambda_k2)[0]])[None, :]
    lam = np.ascontiguousarray(lam)
    ln = np.ascontiguousarray(np.stack([f(ln1_g)[0], f(ln1_b)[0], f(ln2_g)[0], f(ln2_b)[0]], axis=0))
    tri = np.where(np.arange(128)[:, None] <= np.arange(128)[None, :], 0.0, NEG).astype(np.float32)

    in_maps = []
    poss = []
    for c in range(8):
        b, r = c // 4, c % 4
        pos = (512 * np.arange(8)[:, None] + 128 * r + np.arange(128)[None, :]).reshape(-1)
        poss.append(pos)
        xo = x[b][pos]
        cosT, sinT = _rope_tables(pos)
        cbp = np.zeros((128, 640), np.float32)
        cbp[:, 0:128] = np.eye(128, dtype=np.float32)
        for r2 in range(4):
            blk = cbp[:, 128 + 128 * r2:256 + 128 * r2]
            if r2 > r:
                blk[:] = NEG
            elif r2 == r:
                blk[:] = tri
        in_maps.append({
            "xT": np.ascontiguousarray(xo.T), "xres": np.ascontiguousarray(xo),
            "w_in": w_in0, "w_uq": w_uq_p, "w_ukv": w_ukv_p, "w_a": w_a0, "w_b": w_b0, "w_out": w_out0,
            "w_fi": w_fi_p, "w_fd": w_fd0,
            "cs": np.ascontiguousarray(np.concatenate([cosT, sinT], axis=1)),
            "cf": cfp, "cb": cbp, "lam": lam, "ln": ln,
        })
    return in_maps, poss


def kernel(**inputs):
    in_maps, poss = _prepare(**inputs)
    if "nc" not in _NC_CACHE:
        _NC_CACHE["nc"] = build_program()
    res = run_bass_kernel_spmd(_NC_CACHE["nc"], in_maps, core_ids=list(range(8)))
    out = np.empty((2, SEQ, D), np.float32)
    for c in range(8):
        out[c // 4][poss[c]] = np.asarray(res.results[c]["out"], dtype=np.float32)
    return out
```

```python
from contextlib import ExitStack

import numpy as np
import concourse.bass as bass
import concourse.mybir as mybir
from concourse.bass_utils import run_bass_kernel_spmd

F32 = mybir.dt.float32
BF16 = mybir.dt.bfloat16
AF = mybir.ActivationFunctionType
ALU = mybir.AluOpType
AX = mybir.AxisListType

D = 2048
SEQ = 4096
NT = 1024
DFF = 5632
ALPHA = 2.0 ** 0.25
LAMBDA_INIT = 0.2
MLA_SCALE = 192.0 ** -0.5
DIFF_SCALE = 64.0 ** -0.5
RMS_EPS = 1e-6
SUBLN_EPS = 1e-5
LN_EPS = 1e-5
NEG = -30000.0
WSLOT = 4096
NWS = 6

ENGINES = ("pe", "act", "dve", "pool", "sp")


class Tile:
    def __init__(self, name, global_=False):
        self.name = name
        self.writers = []
        self.readers = []
        self.gen_deps = []
        self.sem = None
        self.count = 0
        self.global_ = global_


class Op:
    __slots__ = ("eng", "emit", "deps", "signal", "idx", "sigval", "is_dma", "tile", "semval", "inc", "nowait")

    def __init__(self, eng, emit):
        self.eng = eng
        self.emit = emit
        self.deps = []
        self.signal = False
        self.idx = -1
        self.sigval = 0
        self.is_dma = False
        self.tile = None
        self.semval = 0
        self.inc = 16
        self.nowait = False


class FW:
    def __init__(self, nc, es):
        self.nc = nc
        self.es = es
        self.ops = {e: [] for e in ENGINES}
        self.engsem = {e: es.enter_context(nc.semaphore("sem_" + e)) for e in ("pe", "act", "dve", "pool")}
        self.pending_dma = []
        self.barrier_deps = []
        self.need_barrier = {e: False for e in ENGINES}
        self.nsem = 4

    def _dedupe(self, deps, op):
        best = {}
        for d in deps:
            if d is None or d is op:
                continue
            if d.is_dma:
                key = ("t", id(d.tile))
                if key not in best or best[key].semval < d.semval:
                    best[key] = d
            else:
                if d.eng == "pe" and op.eng == "pe":
                    continue
                key = ("e", d.eng)
                if key not in best or best[key].idx < d.idx:
                    best[key] = d
        return list(best.values())

    def _add(self, op, reads, writes, partial, extra):
        deps = list(extra)
        if self.need_barrier[op.eng] and not (op.eng == "pool" and op.is_dma):
            deps += self.barrier_deps
            self.need_barrier[op.eng] = False
        for t in reads:
            deps += t.writers
        for t in writes:
            deps += t.readers
            deps += t.gen_deps
            if not partial:
                deps += t.writers
        op.deps = self._dedupe(deps, op)
        for d in op.deps:
            if not d.is_dma:
                d.signal = True
        for t in reads:
            if t not in writes:
                t.readers.append(op)
        for t in writes:
            if t.readers:
                t.gen_deps = [r for r in t.readers if r is not op]
                t.writers = [op]
                t.readers = []
            elif partial:
                t.writers.append(op)
            else:
                t.gen_deps = []
                t.writers = [op]
        op.idx = len(self.ops[op.eng])
        self.ops[op.eng].append(op)
        return op

    def op(self, eng, emit, reads=(), writes=(), partial=False, extra=()):
        return self._add(Op(eng, emit), list(reads), list(writes), partial, extra)

    def dma(self, queue, emit, dst, src=None, partial=True, extra=(), inc=16, after_barrier=False, sem_on_src=False,
            sem_tile=None):
        op = Op(queue, emit)
        op.is_dma = True
        st = sem_tile if sem_tile is not None else (src if sem_on_src else dst)
        op.tile = st
        op.inc = inc
        if st.sem is None:
            st.sem = self.es.enter_context(self.nc.semaphore("ts_" + st.name))
            self.nsem += 1
        st.count += inc
        op.semval = st.count
        ex = list(extra)
        if after_barrier:
            ex += self.barrier_deps
        self._add(op, [src] if src is not None else [], [dst], partial, ex)
        if not dst.global_:
            self.pending_dma.append(op)
        return op

    def barrier(self):
        deps = []
        for e in ("pe", "act", "dve", "pool"):
            last = None
            for o in reversed(self.ops[e]):
                if not o.is_dma and o.emit is not None:
                    last = o
                    break
            if last is not None:
                last.signal = True
                deps.append(last)
        deps += self.pending_dma
        self.pending_dma = []
        self.barrier_deps = deps
        for e in ("pe", "act", "dve", "sp", "pool"):
            self.need_barrier[e] = True

    def final_wait(self, eng, deps):
        op = Op(eng, None)
        op.deps = self._dedupe(list(deps), op)
        for d in op.deps:
            if not d.is_dma:
                d.signal = True
        op.idx = len(self.ops[eng])
        self.ops[eng].append(op)

    def finalize(self):
        for e in ENGINES:
            c = 0
            for op in self.ops[e]:
                if op.signal and not op.is_dma:
                    c += 1
                    op.sigval = c

    def emit(self, eng, handle):
        waited = {}
        for op in self.ops[eng]:
            for d in op.deps:
                if d.is_dma:
                    sem, val = d.tile.sem, d.semval
                else:
                    sem, val = self.engsem[d.eng], d.sigval
                k = id(sem)
                if waited.get(k, 0) >= val:
                    continue
                waited[k] = val
                handle.wait_ge(sem, val)
            if op.emit is None:
                continue
            ins = op.emit(handle)
            if op.is_dma:
                if op.inc == 16:
                    ins.then_inc(op.tile.sem, 16)
                else:
                    ins.then_inc(op.tile.sem)
            elif op.signal:
                ins.then_inc(self.engsem[eng], 1)


class Arena:
    def __init__(self, nc, nbytes):
        self.nbytes = nbytes
        self.t = nc.alloc_sbuf_tensor("arena", [128, nbytes // 4], F32)
        self.ap = self.t.ap()
        self.items = []

    def alloc(self, nbytes, p0, p1):
        nbytes = (nbytes + 63) // 64 * 64
        conf = sorted([(o, n) for (o, n, a, b) in self.items if not (b < p0 or a > p1)])
        off = 0
        for (o, n) in conf:
            if off + nbytes <= o:
                break
            off = max(off, o + n)
        assert off + nbytes <= self.nbytes, ("arena overflow", off, nbytes, p0, p1)
        self.items.append((off, nbytes, p0, p1))
        return off

    def f32(self, n, p0, p1):
        off = self.alloc(4 * n, p0, p1)
        return self.ap[:, off // 4: off // 4 + n]

    def bf16(self, n, p0, p1):
        off = self.alloc(2 * n, p0, p1)
        return self.ap[:, off // 4: off // 4 + n // 2].bitcast(BF16)


class _Stop(Exception):
    pass


def build_program(debug=False, stop_after=99):
    nc = bass.Bass("TRN2", target_bir_lowering=False)
    es = ExitStack()
    with es:
        def din(name, shape, dt=F32):
            return nc.dram_tensor(name, list(shape), dt, kind="ExternalInput").ap()

        xT_d = din("xT", [D, NT])
        xres_d = din("xres", [NT, D])
        w_in_d = din("w_in", [D, 8256])
        w_uq_d = din("w_uq", [512, 1536])
        w_ukv_d = din("w_ukv", [512, 2048])
        w_a_d = din("w_a", [1024, D])
        w_b_d = din("w_b", [1024, D])
        w_out_d = din("w_out", [D, D])
        w_fi_d = din("w_fi", [D, 2 * DFF])
        w_fd_d = din("w_fd", [DFF, D])
        cs_d = din("cs", [128, 2 * NT])
        cf_d = din("cf", [128, 144])
        cb_d = din("cb", [128, 640])
        lam_d = din("lam", [1, 256])
        ln_d = din("ln", [4, D])
        out_d = nc.dram_tensor("out", [NT, D], F32, kind="ExternalOutput").ap()
        GN = {"ckv": 512, "kpe": 128, "dk0": 512, "dk1": 512, "dv0": 512, "dv1": 512}
        gin = {k: nc.dram_tensor("gin_" + k, [n, NT], BF16) for k, n in GN.items()}
        gout = {k: nc.dram_tensor("gout_" + k, [4 * n, NT], BF16) for k, n in GN.items()}

        ar = Arena(nc, 207 * 1024)
        fw = FW(nc, es)

        banks = []
        bank_t = []
        for i in range(8):
            p = es.enter_context(nc.psum_tensor("ps%d" % i, [128, 512], F32))
            banks.append(p[:])
            bank_t.append(Tile("ps%d" % i))
        bank_rr = [0]

        def next_bank(choices=None):
            ch = choices if choices is not None else list(range(8))
            b = ch[bank_rr[0] % len(ch)]
            bank_rr[0] += 1
            return b

        def MM(out, lhsT, rhs, start, stop, reads, wt, partial=False):
            return fw.op("pe", lambda e: e.matmul(out, lhsT, rhs, start=start, stop=stop),
                         reads=reads, writes=[wt], partial=partial)

        def TR(out, in_, reads, wt, partial=False):
            return fw.op("pe", lambda e: e.transpose(out, in_, ident), reads=reads + [T_constb], writes=[wt],
                         partial=partial)

        def ACT(out, in_, func, reads, writes, scale=1.0, bias=0.0, partial=False):
            return fw.op("act", lambda e: e.activation(out=out, in_=in_, func=func, bias=bias, scale=scale),
                         reads=reads, writes=writes, partial=partial)

        def ACOPY(out, in_, reads, writes, partial=False):
            return fw.op("act", lambda e: e.copy(out=out, in_=in_), reads=reads, writes=writes, partial=partial)

        def VCOPY(out, in_, reads, writes, partial=False):
            return fw.op("dve", lambda e: e.tensor_copy(out=out, in_=in_), reads=reads, writes=writes, partial=partial)

        def VTT(out, in0, in1, op, reads, writes, partial=False):
            return fw.op("dve", lambda e: e.tensor_tensor(out=out, in0=in0, in1=in1, op=op),
                         reads=reads, writes=writes, partial=partial)

        def PTT(out, in0, in1, op, reads, writes, partial=False):
            return fw.op("pool", lambda e: e.tensor_tensor(out=out, in0=in0, in1=in1, op=op),
                         reads=reads, writes=writes, partial=partial)

        def VTS(out, in0, s1, s2, op0, op1, reads, writes, partial=False):
            if op1 is None:
                return fw.op("dve", lambda e: e.tensor_scalar(out=out, in0=in0, scalar1=s1, scalar2=None, op0=op0),
                             reads=reads, writes=writes, partial=partial)
            return fw.op("dve", lambda e: e.tensor_scalar(out=out, in0=in0, scalar1=s1, scalar2=s2, op0=op0, op1=op1),
                         reads=reads, writes=writes, partial=partial)

        def VSTT(out, in0, scalar, in1, op0, op1, reads, writes, partial=False):
            return fw.op("dve", lambda e: e.scalar_tensor_tensor(out=out, in0=in0, scalar=scalar, in1=in1,
                                                                 op0=op0, op1=op1),
                         reads=reads, writes=writes, partial=partial)

        def VRECIP(out, in_, reads, writes):
            return fw.op("dve", lambda e: e.reciprocal(out=out, in_=in_), reads=reads, writes=writes)

        def DMA(queue, out, in_, dst, src=None, **kw):
            return fw.dma(queue, lambda e: e.dma_start(out=out, in_=in_), dst, src=src, **kw)

        dbg_out = {}

        def dump(name, ap, tile, shape, dt):
            if not debug:
                return
            d = nc.dram_tensor("dbg_" + name, list(shape), dt, kind="ExternalOutput").ap()
            tt = Tile("dbg_" + name)
            dbg_out[name] = DMA("sp", d, ap, tt, src=tile)

        def phase_end(n):
            if stop_after == n:
                raise _Stop()

        cf = ar.f32(144, 0, 99)
        cb = ar.bf16(640, 0, 99)
        ones_f = ar.f32(128, 0, 99)
        ones_b = ar.bf16(128, 0, 99)
        sm = ar.f32(16, 0, 99)
        T_const = Tile("consts")
        T_constb = Tile("constsb")
        T_ones = Tile("ones")
        T_sm = Tile("small")
        RT = cf[:, 0:128]
        ident = cb[:, 0:128]
        wsl_ap = [ar.bf16(WSLOT, 0, 99) for _ in range(NWS)]
        wsl_t = [Tile("ws%d" % i, global_=True) for i in range(NWS)]
        ws_rr = [0]

        def ws_acquire():
            i = ws_rr[0] % NWS
            ws_rr[0] += 1
            return i

        def ws_load(slot, eoff, src3, K, C, first):
            dst = wsl_ap[slot][:, eoff:eoff + K * C].rearrange("p (k c) -> p k c", k=K)
            DMA("pool", dst, src3, wsl_t[slot], partial=not first)
            for it in list(ag_pending):
                it[0] -= 1
                if it[0] <= 0:
                    ag_pending.remove(it)
                    it[1]()
            return dst

        ag_pending = []

        def flush_ag(n=None):
            for it in list(ag_pending)[:n]:
                ag_pending.remove(it)
                it[1]()

        def wsrc(w_d, r0, nk, c0, C):
            return w_d[r0:r0 + nk * 128, c0:c0 + C].rearrange("(k p) c -> p k c", p=128)

        def load_w(src3, K, C):
            s = ws_acquire()
            return ws_load(s, 0, src3, K, C, True), s

        kvbuf = ar.bf16(2 * 16384, 5, 6)
        h1 = ar.f32(8 * D, 7, 10).rearrange("p (t n) -> p t n", t=8)
        h1T = ar.bf16(16 * NT, 7, 9).rearrange("p (k t) -> p k t", k=16)
        actT = ar.bf16(11 * NT, 8, 9).rearrange("p (c t) -> p c t", c=11)
        mergedT = ar.bf16(16 * NT, 6, 7).rearrange("p (k t) -> p k t", k=16)
        dqT = ar.bf16(8 * NT, 1, 5).rearrange("p (c t) -> p c t", c=8)
        mlaT = ar.bf16(8 * NT, 4, 6).rearrange("p (c t) -> p c t", c=8)
        diffT = ar.bf16(8 * NT, 5, 6).rearrange("p (c t) -> p c t", c=8)
        qnT = ar.bf16(8 * NT, 1, 4).rearrange("p (c t) -> p c t", c=8)
        qpeT = ar.bf16(4 * NT, 1, 4).rearrange("p (c t) -> p c t", c=4)
        sq = ar.f32(2 * 512, 1, 5).rearrange("p (c n) -> p c n", c=2)
        rstd = ar.f32(512, 1, 5)
        Pb = ar.bf16(6 * 512, 4, 5).rearrange("p (c n) -> p c n", c=6)
        rec = ar.f32(2 * 512, 4, 5).rearrange("p (c n) -> p c n", c=2)

        try:
            lamraw = ar.f32(256, 0, 3)
            cs = ar.f32(2 * NT, 0, 3)
            cosT = cs[:, 0:NT]
            sinT = cs[:, NT:2 * NT]
            xT = ar.bf16(16 * NT, 0, 3).rearrange("p (k t) -> p k t", k=16)
            xT_t = [Tile("xT%d" % i) for i in range(4)]
            T_cs = Tile("cs")
            T_lam = Tile("lamraw")

            DMA("sp", cf, cf_d[:, :], T_const)
            DMA("pool", cb, cb_d[:, :], T_constb)
            DMA("sp", lamraw.rearrange("p (o n) -> p o n", o=1), lam_d[0:1, :].partition_broadcast(128), T_lam)
            DMA("sp", cs, cs_d[:, :], T_cs)
            for i in range(4):
                DMA("pool", xT[:, 4 * i:4 * i + 4, :],
                    xT_d[512 * i:512 * (i + 1), :].rearrange("(k p) t -> p k t", p=128), xT_t[i])
            fw.op("dve", lambda e: e.memset(ones_f, 1.0), writes=[T_ones])
            fw.op("dve", lambda e: e.memset(ones_b, 1.0), writes=[T_ones], partial=True)
            lr = lamraw.rearrange("p (a n) -> p a n", a=4)
            VTT(lr[:, 0, :], lr[:, 0, :], lr[:, 1, :], ALU.mult, [T_lam], [T_lam])
            VTT(lr[:, 2, :], lr[:, 2, :], lr[:, 3, :], ALU.mult, [T_lam], [T_lam])
            fw.op("dve", lambda e: e.reduce_sum(out=sm[:, 2:3], in_=lr[:, 0, :], axis=AX.X), reads=[T_lam], writes=[T_sm])
            fw.op("dve", lambda e: e.reduce_sum(out=sm[:, 3:4], in_=lr[:, 2, :], axis=AX.X), reads=[T_lam, T_sm],
                  writes=[T_sm])
            ACT(sm[:, 4:6], sm[:, 2:4], AF.Exp, [T_sm], [T_sm])
            VTT(sm[:, 6:7], sm[:, 5:6], sm[:, 4:5], ALU.subtract, [T_sm], [T_sm])
            VTS(sm[:, 0:1], sm[:, 6:7], -LAMBDA_INIT, None, ALU.add, None, [T_sm], [T_sm])
            VTS(sm[:, 1:2], cf[:, 136:137], 1.0 - LAMBDA_INIT, None, ALU.mult, None, [T_sm, T_const], [T_sm])
            fw.op("dve", lambda e: e.memset(sm[:, 8:9], RMS_EPS), writes=[T_sm], reads=[T_sm])
            fw.op("dve", lambda e: e.memset(sm[:, 9:10], SUBLN_EPS), writes=[T_sm], reads=[T_sm])
            fw.op("dve", lambda e: e.memset(sm[:, 10:11], LN_EPS), writes=[T_sm], reads=[T_sm])
            eps_rms, eps_sub, eps_ln = sm[:, 8:9], sm[:, 9:10], sm[:, 10:11]
            neglam = sm[:, 0:1]
            gsub = sm[:, 1:2]

            phase_end(0)
            rawf = ar.f32(4 * 512, 1, 3).rearrange("p (c n) -> p c n", c=4)
            rawf_t = [Tile("rawf%d" % i) for i in range(4)]
            sq_t = [Tile("sq%d" % i) for i in range(2)]
            rstd_t = Tile("rstd")
            xf = ar.f32(2 * 512, 1, 3).rearrange("p (c n) -> p c n", c=2)
            xf_t = [Tile("xf%d" % i) for i in range(2)]
            t1 = ar.f32(2 * 512, 1, 3).rearrange("p (c n) -> p c n", c=2)
            t1_t = [Tile("t1%d" % i) for i in range(2)]
            t2 = ar.f32(2 * 512, 1, 3).rearrange("p (c n) -> p c n", c=2)
            t2_t = [Tile("t2%d" % i) for i in range(2)]
            stg = ar.bf16(4 * 512, 1, 3).rearrange("p (c n) -> p c n", c=4)
            stg_t = [Tile("stg%d" % i) for i in range(4)]
            rr = {"sq": 0, "xf": 0, "stg": 0}

            pe_defer = []

            def defer(fn):
                pe_defer.append(fn)

            def flush_deferred():
                pend = pe_defer[:]
                pe_defer.clear()
                for f in pend:
                    f()

            def flush_all():
                while pe_defer:
                    flush_deferred()

            def chain(bank, n_out, nk, lhsT_fn, rhs_fn, reads_fn, col0=0, cont=False):
                pend = pe_defer[:]
                pe_defer.clear()
                for k in range(nk):
                    MM(banks[bank][:, col0:col0 + n_out], lhsT_fn(k), rhs_fn(k), k == 0, k == nk - 1,
                       reads_fn(k), bank_t[bank], partial=(k > 0 or cont))
                for f in pend:
                    f()

            def rope_to(bank, g, dst, dst_t, after=None):
                i = rr["xf"] % 2
                rr["xf"] += 1
                ACOPY(xf[:, i, :], banks[bank][:, 0:512], [bank_t[bank]], [xf_t[i]])

                def part2():
                    b2 = next_bank()
                    MM(banks[b2][:, 0:512], RT, xf[:, i, :], True, True, [T_const, xf_t[i]], bank_t[b2])
                    VTT(t1[:, i, :], xf[:, i, :], cosT[:, g * 512:(g + 1) * 512], ALU.mult, [xf_t[i], T_cs], [t1_t[i]])
                    VTT(t2[:, i, :], banks[b2][:, 0:512], sinT[:, g * 512:(g + 1) * 512], ALU.mult,
                        [bank_t[b2], T_cs], [t2_t[i]])
                    VTT(dst, t1[:, i, :], t2[:, i, :], ALU.add, [t1_t[i], t2_t[i]], [dst_t], partial=True)
                    if after is not None:
                        after()
                defer(part2)

            def latent_norm(wa, sa_, wb, sb_, gain_col0, g, src, src_t, dst_fn, after=None):
                sbk = next_bank()

                def stat_mm(i, c):
                    MM(banks[sbk][:, 0:512], ones_f, sq[:, i, :], c == 0, c == 3, [T_ones, sq_t[i]], bank_t[sbk],
                       partial=(c > 0))

                for c in range(4):
                    w_, s_ = (wa, sa_) if c < 2 else (wb, sb_)
                    cc = c % 2
                    b = next_bank()
                    chain(b, 512, 16, lambda k: w_[:, k, cc * 128:(cc + 1) * 128],
                          lambda k: src[:, k, g * 512:(g + 1) * 512], lambda k: [src_t[k // 4], wsl_t[s_]])
                    ACOPY(rawf[:, c, :], banks[b][:, 0:512], [bank_t[b]], [rawf_t[c]])
                    i = rr["sq"] % 2
                    rr["sq"] += 1
                    ACT(sq[:, i, :], banks[b][:, 0:512], AF.Square, [bank_t[b]], [sq_t[i]])
                    defer(lambda i=i, c=c: stat_mm(i, c))

                def final():
                    ACT(rstd, banks[sbk][:, 0:512], AF.Ln, [bank_t[sbk], T_sm], [rstd_t], scale=1.0 / 512.0,
                        bias=eps_rms)
                    ACT(rstd, rstd, AF.Exp, [rstd_t], [rstd_t], scale=-0.5)
                    for c in range(4):
                        dst, dt_ = dst_fn(c)
                        VSTT(dst, rawf[:, c, :], cf[:, gain_col0 + c:gain_col0 + c + 1], rstd, ALU.mult, ALU.mult,
                             [rawf_t[c], rstd_t, T_const], [dt_], partial=True)
                    if after is not None:
                        after()
                defer(final)

            def new_stg():
                i = rr["stg"] % 4
                rr["stg"] += 1
                return i

            T_gin = {k: Tile("gin_" + k) for k in GN}
            T_gout = {k: Tile("gout_" + k) for k in GN}

            def all_gather(k):
                def issue():
                    fw.dma("pool", lambda e: e.collective_compute(
                        "AllGather", ALU.bypass, replica_groups=RG, ins=[gin[k].ap().opt()],
                        outs=[gout[k].ap().opt()]), T_gout[k], src=T_gin[k], partial=False, inc=1)
                ag_pending.append([10 ** 9, issue])
            RG = [[0, 1, 2, 3], [4, 5, 6, 7]]

            w_ckv, s_ckv = load_w(wsrc(w_in_d, 0, 16, 512, 256), 16, 256)
            w_ckv2, s_ckv2 = load_w(wsrc(w_in_d, 0, 16, 768, 256), 16, 256)
            def ckv_group(g):
                used = []

                def dst_fn(c):
                    i = new_stg()
                    used.append((c, i))
                    return stg[:, i, :], stg_t[i]

                def after():
                    for (c, i) in used:
                        DMA("sp", gin["ckv"].ap()[c * 128:(c + 1) * 128, g * 512:(g + 1) * 512], stg[:, i, :],
                            T_gin["ckv"], src=stg_t[i], sem_on_src=True)
                latent_norm(w_ckv, s_ckv, w_ckv2, s_ckv2, 132, g, xT, xT_t, dst_fn, after)
            for g in range(2):
                ckv_group(g)
            phase_end(0.1)
            s_kr = ws_acquire()
            w_kr = wsl_ap[s_kr][:, 0:16 * 128].rearrange("p (k c) -> p k c", k=16)
            for h2 in range(2):
                DMA("pool", w_kr[:, :, h2 * 64:(h2 + 1) * 64], wsrc(w_in_d, 0, 16, 1024, 64), wsl_t[s_kr],
                    partial=(h2 > 0))
            def roped_store(b, g, key, r0):
                ii = new_stg()

                def dst_after():
                    DMA("sp", gin[key].ap()[r0:r0 + 128, g * 512:(g + 1) * 512], stg[:, ii, :], T_gin[key],
                        src=stg_t[ii], sem_on_src=True)
                rope_to(b, g, stg[:, ii, :], stg_t[ii], after=dst_after)

            for g in range(2):
                b = next_bank()
                chain(b, 512, 16, lambda k: w_kr[:, k, :], lambda k: xT[:, k, g * 512:(g + 1) * 512],
                      lambda k: [xT_t[k // 4], wsl_t[s_kr]])
                roped_store(b, g, "kpe", 0)
            flush_all()
            all_gather("ckv")
            phase_end(0.2)
            all_gather("kpe")
            phase_end(0.3)
            for cg in range(4):
                w_dk, s_dk = load_w(wsrc(w_in_d, 0, 16, 2112 + cg * 256, 256), 16, 256)
                if cg == 3:
                    flush_ag(2)
                for cc in range(2):
                    c = cg * 2 + cc
                    for g in range(2):
                        b = next_bank()
                        chain(b, 512, 16, lambda k: w_dk[:, k, cc * 128:(cc + 1) * 128],
                              lambda k: xT[:, k, g * 512:(g + 1) * 512], lambda k: [xT_t[k // 4], wsl_t[s_dk]])
                        roped_store(b, g, "dk%d" % (c // 4), (c % 4) * 128)
                if cg % 2 == 1:
                    flush_all()
                    all_gather("dk%d" % (cg // 2))
            phase_end(0.4)
            for cg in range(4):
                w_dv, s_dv = load_w(wsrc(w_in_d, 0, 16, 3136 + cg * 256, 256), 16, 256)
                for tp in range(4):
                    b = next_bank()
                    for hh in range(2):
                        t = tp * 2 + hh
                        chain(b, 256, 16, lambda k: xT[:, k, t * 128:(t + 1) * 128], lambda k: w_dv[:, k, :],
                              lambda k: [xT_t[k // 4], wsl_t[s_dv]], col0=hh * 256, cont=(hh > 0))
                    i = new_stg()
                    ACOPY(stg[:, i, :], banks[b][:, 0:512], [bank_t[b]], [stg_t[i]])
                    for hh in range(2):
                        t = tp * 2 + hh
                        kk_ = "dv%d" % (t // 4)
                        DMA("sp", gin[kk_].ap()[(t % 4) * 128:(t % 4 + 1) * 128, cg * 256:(cg + 1) * 256],
                            stg[:, i, hh * 256:(hh + 1) * 256], T_gin[kk_], src=stg_t[i], sem_on_src=True)
            phase_end(0.5)
            all_gather("dv0")
            all_gather("dv1")

            phase_end(1)
            cqnT = ar.bf16(4 * NT, 1, 3).rearrange("p (c t) -> p c t", c=4)
            cqn_t = [Tile("cqnT")]
            qn_t = [Tile("qnT%d" % h) for h in range(8)]
            qpe_t = [Tile("qpeT%d" % j) for j in range(4)]
            dq_t = [Tile("dqT%d" % h) for h in range(8)]

            w_cq, s_cq = load_w(wsrc(w_in_d, 0, 16, 0, 256), 16, 256)
            w_cq2, s_cq2 = load_w(wsrc(w_in_d, 0, 16, 256, 256), 16, 256)
            flush_ag(2)
            for g in range(2):
                latent_norm(w_cq, s_cq, w_cq2, s_cq2, 128, g, xT, xT_t,
                            lambda c, g=g: (cqnT[:, c, g * 512:(g + 1) * 512], cqn_t[0]))
            flush_all()
            w_uqn, s_uqn = load_w(wsrc(w_uq_d, 0, 4, 0, 1024), 4, 1024)
            w_uqr, s_uqr = load_w(wsrc(w_uq_d, 0, 4, 1024, 512), 4, 512)
            for h in range(8):
                for g in range(2):
                    b = next_bank()
                    chain(b, 512, 4, lambda k: w_uqn[:, k, h * 128:(h + 1) * 128],
                          lambda k: cqnT[:, k, g * 512:(g + 1) * 512], lambda k: [cqn_t[0], wsl_t[s_uqn]])
                    ACOPY(qnT[:, h, g * 512:(g + 1) * 512], banks[b][:, 0:512], [bank_t[b]], [qn_t[h]], partial=True)
            for j in range(4):
                for g in range(2):
                    b = next_bank()
                    chain(b, 512, 4, lambda k: w_uqr[:, k, j * 128:(j + 1) * 128],
                          lambda k: cqnT[:, k, g * 512:(g + 1) * 512], lambda k: [cqn_t[0], wsl_t[s_uqr]])
                    rope_to(b, g, qpeT[:, j, g * 512:(g + 1) * 512], qpe_t[j])

            phase_end(2)
            for cg in range(4):
                w_dq, s_dq = load_w(wsrc(w_in_d, 0, 16, 1088 + cg * 256, 256), 16, 256)
                if cg == 1:
                    flush_ag()
                for cc in range(2):
                    h = cg * 2 + cc
                    for g in range(2):
                        b = next_bank()
                        chain(b, 512, 16, lambda k: w_dq[:, k, cc * 128:(cc + 1) * 128],
                              lambda k: xT[:, k, g * 512:(g + 1) * 512], lambda k: [xT_t[k // 4], wsl_t[s_dq]])
                        rope_to(b, g, dqT[:, h, g * 512:(g + 1) * 512], dq_t[h])

            flush_all()
            flush_ag()
            phase_end(3)
            fw.barrier()

            ckv_all = ar.bf16(4 * SEQ, 4, 4).rearrange("p (c t) -> p c t", c=4)
            T_ckv_all = Tile("ckv_all")
            kpe_all = ar.bf16(SEQ, 4, 4)
            knT = ar.bf16(2 * SEQ, 4, 4).rearrange("p (c t) -> p c t", c=2)
            kn_t = [Tile("knT%d" % i) for i in range(2)]
            Vp = ar.bf16(32 * 256, 4, 4).rearrange("p (t c) -> p t c", t=32)
            Vp_t = Tile("Vp")
            P_t = [Tile("P%d" % i) for i in range(6)]
            rec_t = [Tile("rec%d" % i) for i in range(2)]
            mla_t = [Tile("mlaT%d" % h) for h in range(8)]
            diff_t = [Tile("diffT%d" % h) for h in range(8)]

            for r_ in range(4):
                DMA("sp", ckv_all[:, :, r_ * NT:(r_ + 1) * NT],
                    gout["ckv"].ap()[r_ * 512:(r_ + 1) * 512, :].rearrange("(c p) t -> p c t", p=128),
                    T_ckv_all, src=T_gout["ckv"])
                DMA("sp", kpe_all[:, r_ * NT:(r_ + 1) * NT], gout["kpe"].ap()[r_ * 128:(r_ + 1) * 128, :],
                    T_ckv_all, src=T_gout["kpe"])
            w_uk, s_uk = load_w(wsrc(w_ukv_d, 0, 4, 0, 1024), 4, 1024)
            w_uv, s_uv = load_w(wsrc(w_ukv_d, 0, 4, 1024, 1024), 4, 1024)

            def steps_for(G):
                st = []
                for r_ in range(4):
                    for j_ in range(4 * G + 4):
                        a = j_ - 4 * G
                        st.append((r_, j_, 128 * a if a >= 0 else 0, a >= 0))
                return st

            S_BANKS_MLA = [0, 1, 2]
            O_BANK, SUM_BANK = 3, 4
            PROD_BANKS = [5, 6, 7]
            p_rr = [0]
            rec_rr = [0]

            def mla_attention(h, hh, ks, G):
                j, eh = h // 2, h % 2
                steps = steps_for(G)
                ns = len(steps)

                def emit_S(s):
                    r_, j_, col0, dg = steps[s]
                    n = 512 - col0
                    tok = r_ * NT + j_ * 128
                    sbk = S_BANKS_MLA[s % 3]
                    q0 = G * 512 + col0
                    MM(banks[sbk][:, 0:n], knT[:, ks, tok:tok + 128], qnT[:, h, q0:q0 + n], True, False,
                       [kn_t[ks], qn_t[h]], bank_t[sbk])
                    MM(banks[sbk][:, 0:n], kpe_all[eh * 64:(eh + 1) * 64, tok:tok + 128],
                       qpeT[eh * 64:(eh + 1) * 64, j, q0:q0 + n], False, not dg,
                       [T_ckv_all, qpe_t[j]], bank_t[sbk], partial=True)
                    if dg:
                        MM(banks[sbk][:, 0:128], ident, cb[:, 128 + 128 * r_:256 + 128 * r_], False, True,
                           [T_constb], bank_t[sbk], partial=True)

                emit_S(0)
                if ns > 1:
                    emit_S(1)
                for s in range(ns):
                    r_, j_, col0, dg = steps[s]
                    n = 512 - col0
                    sbk = S_BANKS_MLA[s % 3]
                    pi = p_rr[0] % 6
                    p_rr[0] += 1
                    ACT(Pb[:, pi, 0:n], banks[sbk][:, 0:n], AF.Exp, [bank_t[sbk]], [P_t[pi]], scale=MLA_SCALE)
                    if s + 2 < ns:
                        emit_S(s + 2)
                    tt = r_ * 8 + j_
                    MM(banks[O_BANK][:, col0:512], Vp[:, tt, hh * 128:(hh + 1) * 128], Pb[:, pi, 0:n], s == 0, s == ns - 1,
                       [Vp_t, P_t[pi]], bank_t[O_BANK], partial=(s > 0))
                    MM(banks[SUM_BANK][:, col0:512], ones_b, Pb[:, pi, 0:n], s == 0, s == ns - 1,
                       [T_ones, P_t[pi]], bank_t[SUM_BANK], partial=(s > 0))
                ri = rec_rr[0] % 2
                rec_rr[0] += 1
                ACT(rec[:, ri, :], banks[SUM_BANK][:, 0:512], AF.Ln, [bank_t[SUM_BANK]], [rec_t[ri]])
                ACT(rec[:, ri, :], rec[:, ri, :], AF.Exp, [rec_t[ri]], [rec_t[ri]], scale=-1.0)
                VTT(mlaT[:, h, G * 512:(G + 1) * 512], banks[O_BANK][:, 0:512], rec[:, ri, :], ALU.mult,
                    [bank_t[O_BANK], rec_t[ri]], [mla_t[h]], partial=True)

            for hp in range(4):
                for tp in range(16):
                    b = next_bank(PROD_BANKS)
                    for hh in range(2):
                        tt = tp * 2 + hh
                        chain(b, 256, 4, lambda k: ckv_all[:, k, tt * 128:(tt + 1) * 128],
                              lambda k: w_uv[:, k, hp * 256:(hp + 1) * 256], lambda k: [T_ckv_all, wsl_t[s_uv]],
                              col0=hh * 256, cont=(hh > 0))
                    VCOPY(Vp[:, 2 * tp:2 * tp + 2, :], banks[b][:, 0:512].rearrange("p (t c) -> p t c", t=2),
                          [bank_t[b]], [Vp_t], partial=True)
                for hh in range(2):
                    h = hp * 2 + hh
                    ks = h % 2
                    for tg in range(8):
                        b = next_bank(PROD_BANKS)
                        chain(b, 512, 4, lambda k: w_uk[:, k, h * 128:(h + 1) * 128],
                              lambda k: ckv_all[:, k, tg * 512:(tg + 1) * 512], lambda k: [T_ckv_all, wsl_t[s_uk]])
                        VCOPY(knT[:, ks, tg * 512:(tg + 1) * 512], banks[b][:, 0:512], [bank_t[b]], [kn_t[ks]],
                              partial=True)
                    for G in range(2):
                        mla_attention(h, hh, ks, G)
            for h in range(8):
                dump("mlaT%d" % h, mlaT[:, h, :], mla_t[h], [128, NT], BF16)

            phase_end(4)
            fw.barrier()

            kvs = [kvbuf[:, sl_ * 16384:(sl_ + 1) * 16384] for sl_ in range(2)]
            dkT_s = [kvs[sl_][:, 0:8192].rearrange("p (c t) -> p c t", c=2) for sl_ in range(2)]
            dvp_s = [kvs[sl_][:, 8192:16384].rearrange("p (t c) -> p t c", t=32) for sl_ in range(2)]
            kv_t = [Tile("kv%d" % i) for i in range(2)]
            xT2 = kvs[0].rearrange("p (k t) -> p k t", k=16)
            T_xT2sem = Tile("xT2sem")
            dsc = ar.f32(3 * 512, 5, 5).rearrange("p (c n) -> p c n", c=3)
            dsc_t = [Tile("dsc%d" % i) for i in range(3)]
            S_BANKS_D = [0, 1, 2]
            STAT_BANK_D = 3
            O1B, O2B, S1B, S2B = 4, 5, 6, 7

            diff_stages = [{}]

            def diff_attention(h, hh, sl, G):
                steps = steps_for(G)
                ns = len(steps)
                cur_stages = diff_stages[0]
                diff_stages[0] = {}

                def emit_S(s):
                    r_, j_, col0, dg = steps[s]
                    n = 512 - col0
                    tok = r_ * NT + j_ * 128
                    q0 = G * 512 + col0
                    for m in range(2):
                        sbk = S_BANKS_D[(2 * s + m) % 3]
                        MM(banks[sbk][:, 0:n], dkT_s[sl][m * 64:(m + 1) * 64, hh, tok:tok + 128],
                           dqT[m * 64:(m + 1) * 64, h, q0:q0 + n], True, not dg, [kv_t[sl], dq_t[h]], bank_t[sbk])
                        if dg:
                            MM(banks[sbk][:, 0:128], ident, cb[:, 128 + 128 * r_:256 + 128 * r_], False, True,
                               [T_constb], bank_t[sbk], partial=True)

                emit_S(0)
                for s in range(ns):
                    r_, j_, col0, dg = steps[s]
                    n = 512 - col0
                    pis = []
                    for m in range(2):
                        sbk = S_BANKS_D[(2 * s + m) % 3]
                        pi = p_rr[0] % 6
                        p_rr[0] += 1
                        pis.append(pi)
                        ACT(Pb[:, pi, 0:n], banks[sbk][:, 0:n], AF.Exp, [bank_t[sbk]], [P_t[pi]], scale=DIFF_SCALE)
                    if s + 1 < ns:
                        emit_S(s + 1)
                    tt = r_ * 8 + j_
                    for m in range(2):
                        pi = pis[m]
                        ob, sb_ = (O1B, S1B) if m == 0 else (O2B, S2B)
                        MM(banks[ob][:, col0:512], dvp_s[sl][:, tt, hh * 128:(hh + 1) * 128], Pb[:, pi, 0:n],
                           s == 0, s == ns - 1, [kv_t[sl], P_t[pi]], bank_t[ob], partial=(s > 0))
                        MM(banks[sb_][:, col0:512], ones_b, Pb[:, pi, 0:n], s == 0, s == ns - 1,
                           [T_ones, P_t[pi]], bank_t[sb_], partial=(s > 0))
                    for f in cur_stages.pop(s, []):
                        f()
                for s_ in sorted(cur_stages):
                    for f in cur_stages[s_]:
                        f()
                VCOPY(dsc[:, 0, :], banks[O1B][:, 0:512], [bank_t[O1B]], [dsc_t[0]])
                VCOPY(dsc[:, 1, :], banks[O2B][:, 0:512], [bank_t[O2B]], [dsc_t[1]])
                VCOPY(rec[:, 0, :], banks[S1B][:, 0:512], [bank_t[S1B]], [rec_t[0]])
                VCOPY(rec[:, 1, :], banks[S2B][:, 0:512], [bank_t[S2B]], [rec_t[1]])
                stb = STAT_BANK_D

                def st_a():
                    ACT(rec[:, 0, :], rec[:, 0, :], AF.Ln, [rec_t[0]], [rec_t[0]])
                    ACT(rec[:, 1, :], rec[:, 1, :], AF.Ln, [rec_t[1]], [rec_t[1]])

                def st_b():
                    ACT(rec[:, 0, :], rec[:, 0, :], AF.Exp, [rec_t[0]], [rec_t[0]], scale=-1.0)
                    ACT(rec[:, 1, :], rec[:, 1, :], AF.Exp, [rec_t[1]], [rec_t[1]], scale=-1.0)
                    VTT(dsc[:, 0, :], dsc[:, 0, :], rec[:, 0, :], ALU.mult, [dsc_t[0], rec_t[0]], [dsc_t[0]])
                    VTT(dsc[:, 1, :], dsc[:, 1, :], rec[:, 1, :], ALU.mult, [dsc_t[1], rec_t[1]], [dsc_t[1]])
                    VSTT(dsc[:, 2, :], dsc[:, 1, :], neglam, dsc[:, 0, :], ALU.mult, ALU.add,
                         [dsc_t[0], dsc_t[1], T_sm], [dsc_t[2]])
                    VTT(sq[:, 0, :], dsc[:, 2, :], dsc[:, 2, :], ALU.mult, [dsc_t[2]], [sq_t[0]])

                def st_c():
                    MM(banks[stb][:, 0:512], ones_f, sq[:, 0, :], True, True, [T_ones, sq_t[0]], bank_t[stb])
                    ACT(rstd, banks[stb][:, 0:512], AF.Ln, [bank_t[stb], T_sm], [rstd_t], scale=1.0 / 128.0,
                        bias=eps_sub)

                def st_d():
                    ACT(rstd, rstd, AF.Exp, [rstd_t], [rstd_t], scale=-0.5)
                    VSTT(diffT[:, h, G * 512:(G + 1) * 512], dsc[:, 2, :], gsub, rstd, ALU.mult, ALU.mult,
                         [dsc_t[2], rstd_t, T_sm], [diff_t[h]], partial=True)
                diff_stages[0] = {1: [st_a], 2: [st_b], 5: [st_c], 7: [st_d]}

            for hp in range(4):
                sl = hp % 2
                for r_ in range(4):
                    kk_ = "dk%d" % (hp // 2)
                    r0_ = r_ * 512 + (hp % 2) * 256
                    DMA("sp", dkT_s[sl][:, :, r_ * NT:(r_ + 1) * NT],
                        gout[kk_].ap()[r0_:r0_ + 256, :].rearrange("(c p) t -> p c t", p=128),
                        kv_t[sl], src=T_gout[kk_])
                    for tf in range(2):
                        kv_ = "dv%d" % tf
                        DMA("sp", dvp_s[sl][:, r_ * 8 + 4 * tf:r_ * 8 + 4 * tf + 4, :],
                            gout[kv_].ap()[r_ * 512:(r_ + 1) * 512, hp * 256:(hp + 1) * 256].rearrange(
                                "(j p) c -> p j c", p=128), kv_t[sl], src=T_gout[kv_])
                if hp == 3:
                    for i in range(4):
                        DMA("pool", xT2[:, 4 * i:4 * i + 4, :],
                            xT_d[512 * i:512 * (i + 1), :].rearrange("(k p) t -> p k t", p=128), kv_t[0],
                            sem_tile=T_xT2sem)
                for hh in range(2):
                    for G in range(2):
                        diff_attention(hp * 2 + hh, hh, sl, G)
            for s_ in sorted(diff_stages[0]):
                for f in diff_stages[0][s_]:
                    f()
            flush_all()
            for h in range(8):
                dump("diffT%d" % h, diffT[:, h, :], diff_t[h], [128, NT], BF16)

            phase_end(5)
            fw.barrier()

            mg_t = [Tile("mg%d" % i) for i in range(16)]
            esc = ar.f32(8 * 512, 6, 6).rearrange("p (c n) -> p c n", c=8)
            esc_t = [Tile("esc%d" % i) for i in range(8)]
            e_rr = [0]
            for fg in range(8):
                s_ab = ws_acquire()
                w_a = ws_load(s_ab, 0, wsrc(w_a_d, 0, 8, fg * 256, 256), 8, 256, True)
                w_b = ws_load(s_ab, 2048, wsrc(w_b_d, 0, 8, fg * 256, 256), 8, 256, False)
                w_ga, s_ga = load_w(wsrc(w_in_d, 0, 16, 4160 + fg * 256, 256), 16, 256)
                w_gb, s_gb = load_w(wsrc(w_in_d, 0, 16, 6208 + fg * 256, 256), 16, 256)
                for cc in range(2):
                    c = fg * 2 + cc
                    for g in range(2):
                        bya, byb, bga, bgb = next_bank(), next_bank(), next_bank(), next_bank()
                        chain(bya, 512, 8, lambda k: w_a[:, k, cc * 128:(cc + 1) * 128],
                              lambda k: mlaT[:, k, g * 512:(g + 1) * 512], lambda k: [mla_t[k], wsl_t[s_ab]])
                        chain(byb, 512, 8, lambda k: w_b[:, k, cc * 128:(cc + 1) * 128],
                              lambda k: diffT[:, k, g * 512:(g + 1) * 512], lambda k: [diff_t[k], wsl_t[s_ab]])
                        chain(bga, 512, 16, lambda k: w_ga[:, k, cc * 128:(cc + 1) * 128],
                              lambda k: xT2[:, k, g * 512:(g + 1) * 512], lambda k: [kv_t[0], wsl_t[s_ga]])
                        chain(bgb, 512, 16, lambda k: w_gb[:, k, cc * 128:(cc + 1) * 128],
                              lambda k: xT2[:, k, g * 512:(g + 1) * 512], lambda k: [kv_t[0], wsl_t[s_gb]])
                        o = 4 * (e_rr[0] % 2)
                        e_rr[0] += 1
                        ACT(esc[:, o, :], banks[bga][:, 0:512], AF.Sigmoid, [bank_t[bga]], [esc_t[o]])
                        ACT(esc[:, o + 1, :], banks[bgb][:, 0:512], AF.Sigmoid, [bank_t[bgb]], [esc_t[o + 1]])
                        VTT(esc[:, o + 2, :], banks[bya][:, 0:512], esc[:, o, :], ALU.mult,
                            [bank_t[bya], esc_t[o]], [esc_t[o + 2]])
                        VTT(esc[:, o + 3, :], banks[byb][:, 0:512], esc[:, o + 1, :], ALU.mult,
                            [bank_t[byb], esc_t[o + 1]], [esc_t[o + 3]])
                        VTT(mergedT[:, c, g * 512:(g + 1) * 512], esc[:, o + 2, :], esc[:, o + 3, :], ALU.add,
                            [esc_t[o + 2], esc_t[o + 3]], [mg_t[c]], partial=True)
            for c in range(16):
                dump("mg%d" % c, mergedT[:, c, :], mg_t[c], [128, NT], BF16)

            phase_end(6)
            fw.barrier()

            h1_t = [Tile("h1_%d" % t) for t in range(8)]
            lngb = ar.f32(2 * D, 7, 10).rearrange("p (c n) -> p c n", c=2)
            sg = ar.f32(2 * 512, 7, 9).rearrange("p (c n) -> p c n", c=2)
            T_lngb = Tile("lngb")
            lnst = ar.f32(64, 7, 10)
            lnst_t = Tile("lnst")
            hb = ar.bf16(D, 7, 9)
            hb_t = Tile("hb")
            h1T_t = [Tile("h1T_%d" % g) for g in range(2)]
            for t in range(8):
                DMA("sp", h1[:, t, :], xres_d[t * 128:(t + 1) * 128, :], h1_t[t], after_barrier=True)
            for i in range(2):
                DMA("sp", lngb[:, i:i + 1, :], ln_d[i:i + 1, :].partition_broadcast(128), T_lngb, after_barrier=True)

            def layer_norm(t, gb, gb_t):
                for c in range(4):
                    cc = c
                    fw.op("dve", lambda e, cc=cc: e.bn_stats(out=lnst[:, 6 * cc:6 * cc + 6],
                                                             in_=h1[:, t, cc * 512:(cc + 1) * 512]),
                          reads=[h1_t[t], lnst_t] if c == 0 else [h1_t[t]], writes=[lnst_t], partial=(c > 0))
                fw.op("dve", lambda e: e.bn_aggr(out=lnst[:, 24:26], in_=lnst[:, 0:24]), reads=[lnst_t], writes=[lnst_t])
                ACT(lnst[:, 26:27], lnst[:, 25:26], AF.Ln, [lnst_t, T_sm], [lnst_t], bias=eps_ln)
                ACT(lnst[:, 26:27], lnst[:, 26:27], AF.Exp, [lnst_t], [lnst_t], scale=-0.5)
                VSTT(lnst[:, 27:28], lnst[:, 24:25], -1.0, lnst[:, 26:27], ALU.mult, ALU.mult, [lnst_t], [lnst_t])
                ACT(h1[:, t, :], h1[:, t, :], AF.Identity, [h1_t[t], lnst_t], [h1_t[t]],
                    scale=lnst[:, 26:27], bias=lnst[:, 27:28])
                VTT(h1[:, t, :], h1[:, t, :], gb[:, 0, :], ALU.mult, [h1_t[t], gb_t], [h1_t[t]])
                VTT(h1[:, t, :], h1[:, t, :], gb[:, 1, :], ALU.add, [h1_t[t], gb_t], [h1_t[t]])

            def ln1_pre(t):
                layer_norm(t, lngb, T_lngb)
                ACOPY(hb, h1[:, t, :], [h1_t[t]], [hb_t])

            def ln1_post(t):
                for kq in range(4):
                    b = next_bank()
                    pb = banks[b].bitcast(BF16)
                    for kk in range(4):
                        k = kq * 4 + kk
                        TR(pb[:, kk * 128:(kk + 1) * 128], hb[:, k * 128:(k + 1) * 128], [hb_t], bank_t[b],
                           partial=(kk > 0))
                    VCOPY(h1T[:, 4 * kq:4 * kq + 4, t * 128:(t + 1) * 128],
                          pb[:, 0:512].rearrange("p (k n) -> p k n", k=4), [bank_t[b]], [h1T_t[t // 4]], partial=True)

            for half in range(2):
                for cg in range(8):
                    w_o, s_o = load_w(wsrc(w_out_d, 0, 16, cg * 256, 256), 16, 256)
                    for t in range(4 * half, 4 * half + 4):
                        b = next_bank()
                        chain(b, 256, 16, lambda k: mergedT[:, k, t * 128:(t + 1) * 128], lambda k: w_o[:, k, :],
                              lambda k: [mg_t[k], wsl_t[s_o]])
                        VSTT(h1[:, t, cg * 256:(cg + 1) * 256], h1[:, t, cg * 256:(cg + 1) * 256], ALPHA,
                             banks[b][:, 0:256], ALU.mult, ALU.add, [bank_t[b], h1_t[t]], [h1_t[t]])
                    if half == 1 and cg % 2 == 1:
                        tt_ = cg // 2
                        if tt_ > 0:
                            ln1_post(tt_ - 1)
                        ln1_pre(tt_)
            ln1_post(3)
            phase_end(7)

            act_t = Tile("actT")
            sg_t = [Tile("sg%d" % i) for i in range(2)]
            sg_rr = [0]
            out_t = [Tile("out%d" % t) for t in range(8)]
            stores = []

            def ln2_store(t):
                layer_norm(t, lngb, T_lngb)
                stores.append(DMA("sp", out_d[t * 128:(t + 1) * 128, :], h1[:, t, :], out_t[t], src=h1_t[t]))

            def ffn_down(q, cg, ts):
                w_d, s_d = load_w(wsrc(w_fd_d, q * 1408, 11, cg * 256, 256), 11, 256)
                for t in ts:
                    b = next_bank()
                    chain(b, 256, 11, lambda k: actT[:, k, t * 128:(t + 1) * 128], lambda k: w_d[:, k, :],
                          lambda k: [act_t, wsl_t[s_d]])
                    hsl = h1[:, t, cg * 256:(cg + 1) * 256]
                    if q == 0:
                        VSTT(hsl, hsl, ALPHA, banks[b][:, 0:256], ALU.mult, ALU.add, [bank_t[b], h1_t[t]], [h1_t[t]])
                    else:
                        VTT(hsl, hsl, banks[b][:, 0:256], ALU.add, [bank_t[b], h1_t[t]], [h1_t[t]])

            def ffn_in(q, fc, g):
                fcg = q * 11 + fc
                w_f, s_f = load_w(wsrc(w_fi_d, 0, 16, fcg * 256, 256), 16, 256)
                gs = [g] if g is not None else [0, 1]
                for g_ in gs:
                    bg, bu = next_bank(), next_bank()
                    chain(bg, 512, 16, lambda k: w_f[:, k, 0:128], lambda k: h1T[:, k, g_ * 512:(g_ + 1) * 512],
                          lambda k: [h1T_t[g_], wsl_t[s_f]])
                    chain(bu, 512, 16, lambda k: w_f[:, k, 128:256], lambda k: h1T[:, k, g_ * 512:(g_ + 1) * 512],
                          lambda k: [h1T_t[g_], wsl_t[s_f]])
                    si = sg_rr[0] % 2
                    sg_rr[0] += 1
                    ACT(sg[:, si, :], banks[bg][:, 0:512], AF.Silu, [bank_t[bg]], [sg_t[si]])
                    VTT(actT[:, fc, g_ * 512:(g_ + 1) * 512], banks[bu][:, 0:512], sg[:, si, :], ALU.mult,
                        [bank_t[bu], sg_t[si]], [act_t], partial=True)

            for q in range(4):
                if q == 0:
                    for fc in range(11):
                        ffn_in(0, fc, 0)
                        if fc % 2 == 1 and fc <= 7:
                            tt_ = 4 + fc // 2
                            if tt_ > 4:
                                ln1_post(tt_ - 1)
                            ln1_pre(tt_)
                        if fc == 9:
                            ln1_post(7)
                            for i in range(2):
                                DMA("sp", lngb[:, i:i + 1, :], ln_d[2 + i:3 + i, :].partition_broadcast(128), T_lngb,
                                    partial=(i > 0))
                    for t in range(8):
                        dump("h1_%d" % t, h1[:, t, :], h1_t[t], [128, D], F32)
                    phase_end(8)
                    for fc in range(11):
                        ffn_in(0, fc, 1)
                else:
                    for fc in range(11):
                        ffn_in(q, fc, None)
                if q < 3:
                    for cg in range(8):
                        ffn_down(q, cg, range(8))
                else:
                    for cg in range(8):
                        ffn_down(q, cg, range(4))
                    for cg in range(8):
                        ffn_down(q, cg, range(4, 8))
                        if cg % 2 == 1:
                            ln2_store(cg // 2)
            phase_end(9)
            for t in range(4, 8):
                ln2_store(t)
            fw.final_wait("sp", stores + list(dbg_out.values()))

        except _Stop:
            fw.final_wait("sp", list(dbg_out.values()))

        fw.finalize()
        block = es.enter_context(nc.Block())

        @block.tensor
        def _(e):
            fw.emit("pe", e)

        @block.scalar
        def _(e):
            fw.emit("act", e)

        @block.vector
        def _(e):
            fw.emit("dve", e)

        @block.gpsimd
        def _(e):
            fw.emit("pool", e)

        @block.sync
        def _(e):
            fw.emit("sp", e)
    return nc


_NC_CACHE = {}


def _rope_tables(pos):
    inv_freq = 1.0 / (10000.0 ** (np.arange(0, 64, 2, dtype=np.float64) / 64.0))
    ang = pos.astype(np.float64)[:, None] * inv_freq[None, :]
    c = np.cos(ang).astype(np.float32).T
    s = np.sin(ang).astype(np.float32).T
    return np.tile(c, (4, 1)), np.tile(s, (4, 1))


def _prepare(x, w_in, mla_q_norm, mla_w_uq, mla_kv_norm, mla_w_ukv,
             diff_lambda_q1, diff_lambda_k1, diff_lambda_q2, diff_lambda_k2, diff_subln,
             w_branch_a, w_branch_b, w_out, ln1_g, ln1_b, w_ffn_in, w_ffn_down, ln2_g, ln2_b):
    f = lambda a: np.ascontiguousarray(np.asarray(a, dtype=np.float32))
    x = f(x)
    w_in0 = f(w_in)[0]
    uq = f(mla_w_uq)[0].reshape(512, 8, 192)
    w_uq_p = np.ascontiguousarray(np.concatenate([uq[:, :, :128].reshape(512, 1024), uq[:, :, 128:].reshape(512, 512)], axis=1))
    ukv = f(mla_w_ukv)[0].reshape(512, 8, 256)
    w_ukv_p = np.ascontiguousarray(np.concatenate([ukv[:, :, :128].reshape(512, 1024), ukv[:, :, 128:].reshape(512, 1024)], axis=1))
    fi = f(w_ffn_in)[0]
    w_fi_p = np.ascontiguousarray(
        np.stack([fi[:, :DFF].reshape(D, 44, 128), fi[:, DFF:].reshape(D, 44, 128)], axis=2).reshape(D, 2 * DFF))
    w_a0, w_b0, w_out0, w_fd0 = f(w_branch_a)[0], f(w_branch_b)[0], f(w_out)[0], f(w_ffn_down)[0]

    RT = np.zeros((128, 128), np.float32)
    for m in range(128):
        if m % 64 < 32:
            RT[m + 32, m] = -1.0
        else:
            RT[m - 32, m] = 1.0
    cfp = np.zeros((128, 144), np.float32)
    cfp[:, 0:128] = RT
    cfp[:, 128:132] = f(mla_q_norm)[0].reshape(4, 128).T
    cfp[:, 132:136] = f(mla_kv_norm)[0].reshape(4, 128).T
    cfp[:, 136] = f(diff_subln)[0]
    lam = np.concatenate([f(diff_lambda_q1)[0], f(diff_lambda_k1)[0], f(diff_lambda_q2)[0], f(diff_lambda_k2)[0]])[None, :]
    lam = np.ascontiguousarray(lam)
    ln = np.ascontiguousarray(np.stack([f(ln1_g)[0], f(ln1_b)[0], f(ln2_g)[0], f(ln2_b)[0]], axis=0))
    tri = np.where(np.arange(128)[:, None] <= np.arange(128)[None, :], 0.0, NEG).astype(np.float32)

    in_maps = []
    poss = []
    for c in range(8):
        b, r = c // 4, c % 4
        pos = (512 * np.arange(8)[:, None] + 128 * r + np.arange(128)[None, :]).reshape(-1)
        poss.append(pos)
        xo = x[b][pos]
        cosT, sinT = _rope_tables(pos)
        cbp = np.zeros((128, 640), np.float32)
        cbp[:, 0:128] = np.eye(128, dtype=np.float32)
        for r2 in range(4):
            blk = cbp[:, 128 + 128 * r2:256 + 128 * r2]
            if r2 > r:
                blk[:] = NEG
            elif r2 == r:
                blk[:] = tri
        in_maps.append({
            "xT": np.ascontiguousarray(xo.T), "xres": np.ascontiguousarray(xo),
            "w_in": w_in0, "w_uq": w_uq_p, "w_ukv": w_ukv_p, "w_a": w_a0, "w_b": w_b0, "w_out": w_out0,
            "w_fi": w_fi_p, "w_fd": w_fd0,
            "cs": np.ascontiguousarray(np.concatenate([cosT, sinT], axis=1)),
            "cf": cfp, "cb": cbp, "lam": lam, "ln": ln,
        })
    return in_maps, poss


def kernel(**inputs):
    in_maps, poss = _prepare(**inputs)
    if "nc" not in _NC_CACHE:
        _NC_CACHE["nc"] = build_program()
    res = run_bass_kernel_spmd(_NC_CACHE["nc"], in_maps, core_ids=list(range(8)))
    out = np.empty((2, SEQ, D), np.float32)
    for c in range(8):
        out[c // 4][poss[c]] = np.asarray(res.results[c]["out"], dtype=np.float32)
    return out
```
